# Optimizing a Trainium2 kernel written in Bass

```python
import math, functools
import jax, jax.numpy as jnp
from jax import lax
import numpy as np

D_MODEL = 1024
BATCH = 4
SEQ = 8192
DEPTH = 2
DEC_BATCH = 16
DEC_SEQ = 32
PAST_LEN = 4096

CHUNK = 64
H_A = 8
DK_A = 128
DV_A = 128
QK_A = H_A * DK_A
V_A = H_A * DV_A
C_QKV = 2 * QK_A + V_A
CONV_W = 4
H_B = 8
NOPE = 128
ROPE = 64
V_B = 128
Q_RANK = 384
KV_RANK = 256
ROPE_THETA = 10000.0
ATTN_SCALE = (NOPE + ROPE) ** -0.5
Q_BLOCK = 128
D_FF = -(-8 * D_MODEL // (3 * 256)) * 256
ALPHA = (2 * DEPTH) ** 0.25
BETA_INIT = (8 * DEPTH) ** -0.25
EPS = 1e-6
_SIZES = (C_QKV, V_A, H_A, H_A, Q_RANK, KV_RANK, ROPE, D_MODEL, D_MODEL)
SPLIT_POINTS = tuple(sum(_SIZES[:i + 1]) for i in range(len(_SIZES) - 1))
N_IN = sum(_SIZES)

kernel_name = "gdn_mla_parallel_deepnorm_stream_step"


def layer_norm(x, g, b):
    xf = x.astype(jnp.float32)
    mu = jnp.mean(xf, -1, keepdims=True)
    var = jnp.mean(jnp.square(xf - mu), -1, keepdims=True)
    return ((xf - mu) * lax.rsqrt(var + EPS) * g + b).astype(x.dtype)


def rms_norm(x, g):
    xf = x.astype(jnp.float32)
    return (xf * lax.rsqrt(jnp.mean(jnp.square(xf), -1, keepdims=True) + EPS) * g).astype(x.dtype)


def l2_norm(x):
    xf = x.astype(jnp.float32)
    return xf * lax.rsqrt(jnp.sum(jnp.square(xf), -1, keepdims=True) + EPS)


def rope(x, pos):
    half = x.shape[-1] // 2
    inv = ROPE_THETA ** (-jnp.arange(half, dtype=jnp.float32) / half)
    ang = pos.astype(jnp.float32)[:, None] * inv[None, :]
    cos = jnp.cos(ang)[None, :, None, :]
    sin = jnp.sin(ang)[None, :, None, :]
    x1 = x[..., :half].astype(jnp.float32)
    x2 = x[..., half:].astype(jnp.float32)
    return jnp.concatenate([x1 * cos - x2 * sin, x1 * sin + x2 * cos], -1).astype(x.dtype)


def short_conv(x_new, conv_state, w):
    L = x_new.shape[1]
    xcat = jnp.concatenate([conv_state.astype(x_new.dtype), x_new], axis=1)
    y = xcat[:, 0:L] * w[0]
    for j in range(1, CONV_W):
        y = y + xcat[:, j:j + L] * w[j]
    return jax.nn.silu(y), xcat[:, -(CONV_W - 1):]


def gated_delta_rule(q, k, v, g, beta, S0):
    B, L, H, dk = q.shape
    dv = v.shape[-1]
    C = min(CHUNK, L)
    N = L // C

    def blk(t):
        t = t.reshape((B, N, C, H) + t.shape[3:])
        return jnp.moveaxis(t, 3, 1)

    q, k, v = (blk(t.astype(jnp.float32)) for t in (q, k, v))
    g, beta = blk(g), blk(beta)
    G = jnp.cumsum(g, axis=-1)
    diff = G[..., :, None] - G[..., None, :]
    idx = jnp.arange(C)
    strict = idx[:, None] > idx[None, :]
    incl = idx[:, None] >= idx[None, :]
    dec_strict = jnp.exp(jnp.where(strict, diff, -jnp.inf))
    dec_incl = jnp.exp(jnp.where(incl, diff, -jnp.inf))
    kk = jnp.einsum('bhnid,bhnjd->bhnij', k, k)
    A = beta[..., None] * kk * dec_strict
    rhs = jnp.concatenate([v * beta[..., None], k * (beta * jnp.exp(G))[..., None]], -1)
    sol = lax.linalg.triangular_solve(A, rhs, left_side=True, lower=True, unit_diagonal=True)
    u, w = sol[..., :dv], sol[..., dv:]
    qk = jnp.einsum('bhnid,bhnjd->bhnij', q, k) * dec_incl
    qg = q * jnp.exp(G)[..., None]
    kd = k * jnp.exp(G[..., -1:] - G)[..., None]
    gl = jnp.exp(G[..., -1])
    xs = tuple(jnp.moveaxis(t, 2, 0) for t in (u, w, qg, qk, kd, gl))

    def step(S, xs_n):
        u_n, w_n, qg_n, qk_n, kd_n, gl_n = xs_n
        v_new = u_n - jnp.einsum('bhck,bhkv->bhcv', w_n, S)
        o_n = jnp.einsum('bhck,bhkv->bhcv', qg_n, S) + jnp.einsum('bhij,bhjv->bhiv', qk_n, v_new)
        S = S * gl_n[..., None, None] + jnp.einsum('bhck,bhcv->bhkv', kd_n, v_new)
        return S, o_n

    S, o = lax.scan(step, S0.astype(jnp.float32), xs)
    o = jnp.transpose(o, (1, 0, 3, 2, 4)).reshape(B, L, H, dv)
    return o, S


def chunk_causal_attention(q, k, v, q_pos, k_pos):
    B, L, H, Dq = q.shape
    QB = Q_BLOCK if L % Q_BLOCK == 0 else L
    nb = L // QB
    qb = jnp.moveaxis(q.reshape(B, nb, QB, H, Dq), 1, 0)
    pb = q_pos.reshape(nb, QB)
    kc = k_pos // CHUNK

    def one(args):
        qi, pi = args
        s = jnp.einsum('bqhd,bkd->bhqk', qi, k).astype(jnp.float32) * ATTN_SCALE
        s = jnp.where(kc[None, :] <= (pi // CHUNK)[:, None], s, -jnp.inf)
        p = jax.nn.softmax(s, axis=-1).astype(v.dtype)
        return jnp.einsum('bhqk,bkc->bqhc', p, v)

    o = lax.map(one, (qb, pb))
    return jnp.moveaxis(o, 0, 1).reshape(B, L, H, v.shape[-1])


def trunk_layer(x, conv_state, S0, ckv_past, kr_past, w_in, conv_w, a_log, dt_bias, gdn_norm_g, w_oa,
                q_norm_g, w_uq, kv_norm_g, w_ukv, w_ob, w_out, ln1_g, ln1_b, w_gu, w_down, ln2_g, ln2_b):
    B, L, _ = x.shape
    P = ckv_past.shape[1]
    q_pos = P + jnp.arange(L)
    k_pos = jnp.arange(P + L)
    h = x @ w_in
    qkv_raw, z, a_raw, b_raw, c_q, c_kv, k_r, g_a, g_b = jnp.split(h, SPLIT_POINTS, axis=-1)

    qkv, conv_new = short_conv(qkv_raw, conv_state, conv_w)
    q, k, v = jnp.split(qkv, [QK_A, 2 * QK_A], axis=-1)
    q = l2_norm(q.reshape(B, L, H_A, DK_A)) * (DK_A ** -0.5)
    k = l2_norm(k.reshape(B, L, H_A, DK_A))
    v = v.reshape(B, L, H_A, DV_A)
    beta = jax.nn.sigmoid(b_raw.astype(jnp.float32))
    g = -jnp.exp(a_log.astype(jnp.float32)) * jax.nn.softplus(
        a_raw.astype(jnp.float32) + dt_bias.astype(jnp.float32))
    o_a, S_new = gated_delta_rule(q, k, v, g, beta, S0)
    zf = z.astype(jnp.float32).reshape(B, L, H_A, DV_A)
    o_a = (rms_norm(o_a, gdn_norm_g) * jax.nn.silu(zf)).astype(x.dtype).reshape(B, L, V_A)
    y_a = o_a @ w_oa

    qf = (rms_norm(c_q, q_norm_g) @ w_uq).reshape(B, L, H_B, NOPE + ROPE)
    q_nope, q_rope = qf[..., :NOPE], rope(qf[..., NOPE:], q_pos)
    ckv_new = rms_norm(c_kv, kv_norm_g)
    kr_new = rope(k_r[:, :, None, :], q_pos)[:, :, 0]
    w_ukv_h = w_ukv.reshape(KV_RANK, H_B, NOPE + V_B)
    w_uk, w_uv = w_ukv_h[..., :NOPE], w_ukv_h[..., NOPE:]
    q_lat = jnp.einsum('blhn,chn->blhc', q_nope, w_uk)
    keys_lat = jnp.concatenate([ckv_past.astype(x.dtype), ckv_new], axis=1)
    keys_r = jnp.concatenate([kr_past.astype(x.dtype), kr_new], axis=1)
    o_lat = chunk_causal_attention(jnp.concatenate([q_lat, q_rope], -1),
                                   jnp.concatenate([keys_lat, keys_r], -1), keys_lat, q_pos, k_pos)
    o_b = jnp.einsum('blhc,chv->blhv', o_lat, w_uv).reshape(B, L, H_B * V_B)
    y_b = o_b @ w_ob

    m = jax.nn.sigmoid(g_a) * y_a + jax.nn.sigmoid(g_b) * y_b
    x = layer_norm(ALPHA * x + m @ w_out, ln1_g, ln1_b)
    f1, f3 = jnp.split(x @ w_gu, 2, axis=-1)
    x = layer_norm(ALPHA * x + (jax.nn.silu(f1) * f3) @ w_down, ln2_g, ln2_b)
    return x, conv_new, S_new.astype(S0.dtype), ckv_new, kr_new


def setup_inputs(seed: int = 0) -> dict:
    key = jax.random.key(seed)
    ks = iter(jax.random.split(key, 32))

    def nrm(shape, scale):
        return jax.random.normal(next(ks), shape, jnp.float32) * scale

    dt = jax.random.uniform(next(ks), (DEPTH, H_A), jnp.float32, 0.001, 0.1)
    return {
        "x_prompt": nrm((BATCH, SEQ, D_MODEL), 1.0),
        "x_sample": nrm((DEC_BATCH, DEC_SEQ, D_MODEL), 1.0),
        "state_conv": nrm((DEPTH, DEC_BATCH, CONV_W - 1, C_QKV), 1.0),
        "state_gdn": nrm((DEPTH, DEC_BATCH, H_A, DK_A, DV_A), 0.05),
        "cache_ckv": nrm((DEPTH, DEC_BATCH, PAST_LEN, KV_RANK), 1.0),
        "cache_krope": nrm((DEPTH, DEC_BATCH, PAST_LEN, ROPE), 1.0),
        "w_in": nrm((DEPTH, D_MODEL, N_IN), D_MODEL ** -0.5),
        "conv_w": nrm((DEPTH, CONV_W, C_QKV), CONV_W ** -0.5),
        "a_log": jnp.log(jax.random.uniform(next(ks), (DEPTH, H_A), jnp.float32, 1.0, 16.0)),
        "dt_bias": jnp.log(jnp.expm1(dt)),
        "gdn_norm_g": 1.0 + nrm((DEPTH, DV_A), 0.02),
        "w_oa": nrm((DEPTH, V_A, D_MODEL), BETA_INIT * V_A ** -0.5),
        "q_norm_g": 1.0 + nrm((DEPTH, Q_RANK), 0.02),
        "w_uq": nrm((DEPTH, Q_RANK, H_B * (NOPE + ROPE)), Q_RANK ** -0.5),
        "kv_norm_g": 1.0 + nrm((DEPTH, KV_RANK), 0.02),
        "w_ukv": nrm((DEPTH, KV_RANK, H_B * (NOPE + V_B)), KV_RANK ** -0.5),
        "w_ob": nrm((DEPTH, H_B * V_B, D_MODEL), BETA_INIT * (H_B * V_B) ** -0.5),
        "w_out": nrm((DEPTH, D_MODEL, D_MODEL), BETA_INIT * D_MODEL ** -0.5),
        "ln1_g": 1.0 + nrm((DEPTH, D_MODEL), 0.02),
        "ln1_b": nrm((DEPTH, D_MODEL), 0.02),
        "w_gu": nrm((DEPTH, D_MODEL, 2 * D_FF), D_MODEL ** -0.5),
        "w_down": nrm((DEPTH, D_FF, D_MODEL), BETA_INIT * D_FF ** -0.5),
        "ln2_g": 1.0 + nrm((DEPTH, D_MODEL), 0.02),
        "ln2_b": nrm((DEPTH, D_MODEL), 0.02),
    }


def reference(x_prompt, x_sample, state_conv, state_gdn, cache_ckv, cache_krope, w_in, conv_w, a_log,
              dt_bias, gdn_norm_g, w_oa, q_norm_g, w_uq, kv_norm_g, w_ukv, w_ob, w_out, ln1_g, ln1_b,
              w_gu, w_down, ln2_g, ln2_b):
    Bp = x_prompt.shape[0]
    dtp = x_prompt.dtype
    zero_conv = jnp.zeros((Bp, CONV_W - 1, C_QKV), dtp)
    zero_S = jnp.zeros((Bp, H_A, DK_A, DV_A), dtp)
    empty_ckv = jnp.zeros((Bp, 0, KV_RANK), dtp)
    empty_kr = jnp.zeros((Bp, 0, ROPE), dtp)
    yp, ys = x_prompt, x_sample
    pc, pg, pk, pr, sc, sg, sk, sr = [], [], [], [], [], [], [], []
    for l in range(DEPTH):
        wl = (w_in[l], conv_w[l], a_log[l], dt_bias[l], gdn_norm_g[l], w_oa[l], q_norm_g[l], w_uq[l],
              kv_norm_g[l], w_ukv[l], w_ob[l], w_out[l], ln1_g[l], ln1_b[l], w_gu[l], w_down[l],
              ln2_g[l], ln2_b[l])
        yp, c1, g1, k1, r1 = trunk_layer(yp, zero_conv, zero_S, empty_ckv, empty_kr, *wl)
        ys, c2, g2, k2, r2 = trunk_layer(ys, state_conv[l], state_gdn[l], cache_ckv[l], cache_krope[l], *wl)
        pc.append(c1); pg.append(g1); pk.append(k1); pr.append(r1)
        sc.append(c2); sg.append(g2); sk.append(k2); sr.append(r2)
    return (yp, ys, jnp.stack(pc), jnp.stack(pg), jnp.stack(pk), jnp.stack(pr),
            jnp.stack(sc), jnp.stack(sg), jnp.stack(sk), jnp.stack(sr))
```

```python
import contextlib
import numpy as np
import concourse.bass as bass
import concourse.mybir as mybir
from concourse.bass_utils import run_bass_kernel_spmd

F32 = mybir.dt.float32
BF16 = mybir.dt.bfloat16
U8 = mybir.dt.uint8
AF = mybir.ActivationFunctionType
ALU = mybir.AluOpType

ENGS = ("tensor", "vector", "scalar", "gpsimd", "sync")

D = 1024
SEQ = 8192
NSEQ = 4
DEPTH = 2
SB_, SL = 16, 32
PAST = 4096
H = 8
DK = 128
CQKV = 3072
QR, KVR, ROPE, NOPE, VB = 384, 256, 64, 128, 128
DFF = 2816
NFF = DFF // 128
ALPHA = (2 * DEPTH) ** 0.25
EPS = 1e-6
SCALE = (NOPE + ROPE) ** -0.5
TB = 512
NBLK = SEQ // TB
NEGBIG = -30000.0
DBL_DT = BF16


class Buf:
    __slots__ = ("name", "last_w", "readers", "excl")

    def __init__(self, name="", excl=False):
        self.name = name
        self.last_w = None
        self.readers = []
        self.excl = excl


class V:
    __slots__ = ("ap", "bufs")

    def __init__(self, ap, bufs):
        self.ap = ap
        self.bufs = bufs if isinstance(bufs, (list, tuple)) else [bufs]


class Prog:
    N_DMA_SEMS = 24

    def __init__(self, nc):
        self.nc = nc
        self.ops = {e: [] for e in ENGS}
        self.sem_names = list(ENGS) + ["dma%d" % i for i in range(self.N_DMA_SEMS)]
        self.sem_idx = {n: i for i, n in enumerate(self.sem_names)}
        self.sem_tot = [0] * len(self.sem_names)
        self.known = {e: [0] * len(self.sem_names) for e in ENGS}
        self.dma_rrs = {}
        self.n_ops = 0
        self.n_waits = 0

    def _add(self, eng, fn, reads, writes, semidx, inc):
        ns = len(self.sem_names)
        need = [0] * ns
        known = self.known[eng]
        pe_own = self.sem_idx["tensor"] if eng == "tensor" else -1
        ents = []
        for b in reads:
            if b.last_w is not None:
                ents.append(b.last_w)
        for b in writes:
            if b.last_w is not None:
                ents.append(b.last_w)
            ents.extend(b.readers)
        for (s, t, vc) in ents:
            if s == pe_own:
                continue
            if t > need[s]:
                need[s] = t
        waits = [(s, need[s]) for s in range(ns) if need[s] > known[s]]
        for (s, t, vc) in ents:
            if s == pe_own:
                continue
            for i in range(ns):
                if vc[i] > known[i]:
                    known[i] = vc[i]
        for s, t in waits:
            if t > known[s]:
                known[s] = t
        self.sem_tot[semidx] += inc
        ticket = self.sem_tot[semidx]
        vc = list(known)
        if ticket > vc[semidx]:
            vc[semidx] = ticket
        ent = (semidx, ticket, vc)
        wset = set(id(b) for b in writes)
        for b in writes:
            b.last_w = ent
            b.readers = []
        for b in reads:
            if id(b) not in wset:
                b.readers.append(ent)
                if len(b.readers) > 12:
                    latest = {}
                    for r in b.readers:
                        if r[0] not in latest or latest[r[0]][1] < r[1]:
                            latest[r[0]] = r
                    b.readers = list(latest.values())
        self.ops[eng].append((waits, fn, semidx, inc))
        self.n_ops += 1
        self.n_waits += len(waits)

    def op(self, eng, fn, reads=(), writes=()):
        self._add(eng, fn, tuple(reads), tuple(writes), self.sem_idx[eng], 1)

    def dma(self, fn, reads=(), writes=(), eng="sync"):
        lo, hi = (0, 16) if eng == "sync" else (16, self.N_DMA_SEMS)
        k = lo + self.dma_rrs.get(eng, 0)
        self.dma_rrs[eng] = (self.dma_rrs.get(eng, 0) + 1) % (hi - lo)
        semidx = self.sem_idx["dma%d" % k]
        prev = self.sem_tot[semidx]
        if prev > self.known[eng][semidx]:
            self.ops[eng].append(([(semidx, prev)], None, None, 0))
            self.known[eng][semidx] = prev
            self.n_waits += 1
        self._add(eng, fn, tuple(reads), tuple(writes), semidx, 16)

    def finish(self):
        for eng in ENGS:
            waits = []
            for s in range(len(self.sem_names)):
                if self.sem_tot[s] > self.known[eng][s]:
                    waits.append((s, self.sem_tot[s]))
            self.ops[eng].append((waits, None, None, 0))

    def emit(self):
        nc = self.nc
        with contextlib.ExitStack() as st:
            sems = [st.enter_context(nc.semaphore("s_" + n)) for n in self.sem_names]
            block = st.enter_context(nc.Block())

            def run(engname):
                def body(eng):
                    for waits, fn, semidx, inc in self.ops[engname]:
                        for s, t in waits:
                            eng.wait_ge(sems[s], t)
                        if fn is not None:
                            fn(eng).then_inc(sems[semidx], inc)
                return body

            block.tensor(run("tensor"))
            block.vector(run("vector"))
            block.scalar(run("scalar"))
            block.gpsimd(run("gpsimd"))
            block.sync(run("sync"))


class KB:
    def __init__(self, nc):
        self.nc = nc
        self.P = Prog(nc)
        self.cap = 204 * 1024
        self.arena = nc.alloc_sbuf_tensor("arena", [128, self.cap], U8)
        self.off = 0
        self.rr = 0

    def alloc(self, shape, dtype, nb=1, name="t"):
        esz = 4 if dtype == F32 else 2
        n = int(np.prod(shape[1:]))
        nbytes = (n * esz + 31) // 32 * 32
        assert self.off + nbytes <= self.cap, ("SBUF overflow", name, self.off, nbytes)
        ap = self.arena[:, self.off:self.off + n * esz].bitcast(dtype)
        self.off += nbytes
        if len(shape) == 3:
            ap = ap.rearrange("p (a b) -> p a b", a=shape[1])
        elif len(shape) == 4:
            ap = ap.rearrange("p (a b c) -> p a b c", a=shape[1], b=shape[2])
        return Tn(ap, shape, nb, name)

    def _rw(self, ins, outs):
        r, w = [], []
        for v in ins:
            if isinstance(v, V):
                for b in v.bufs:
                    (w if b.excl else r).append(b)
        for v in outs:
            w.extend(v.bufs)
        return r, w

    def act(self, out, in_, func, bias=None, scale=None, accum=None):
        kw = {}
        ins = [in_]
        if bias is not None:
            kw["bias"] = bias.ap if isinstance(bias, V) else bias
            ins.append(bias)
        if scale is not None:
            kw["scale"] = scale.ap if isinstance(scale, V) else scale
            ins.append(scale)
        outs = [out]
        if accum is not None:
            kw["accum_out"] = accum.ap
            outs.append(accum)
        r, w = self._rw(ins, outs)
        o, i = out.ap, in_.ap
        self.P.op("scalar", lambda e: e.activation(out=o, in_=i, func=func, **kw), r, w)

    def tt(self, out, a, b, op, eng="vector"):
        r, w = self._rw([a, b], [out])
        o, x, y = out.ap, a.ap, b.ap
        self.P.op(eng, lambda e: e.tensor_tensor(out=o, in0=x, in1=y, op=op), r, w)

    def ts(self, out, a, s1, op0, s2=None, op1=None, eng="vector"):
        r, w = self._rw([a, s1, s2], [out])
        o, x = out.ap, a.ap
        c1 = s1.ap if isinstance(s1, V) else s1
        c2 = s2.ap if isinstance(s2, V) else s2
        if op1 is None:
            self.P.op(eng, lambda e: e.tensor_scalar(out=o, in0=x, scalar1=c1, scalar2=None, op0=op0), r, w)
        else:
            self.P.op(eng, lambda e: e.tensor_scalar(out=o, in0=x, scalar1=c1, scalar2=c2, op0=op0, op1=op1), r, w)

    def stt(self, out, a, s, b, op0, op1, eng="vector"):
        r, w = self._rw([a, s, b], [out])
        o, x, y = out.ap, a.ap, b.ap
        c = s.ap if isinstance(s, V) else s
        self.P.op(eng, lambda e: e.scalar_tensor_tensor(out=o, in0=x, scalar=c, in1=y, op0=op0, op1=op1), r, w)

    def copy(self, out, in_, eng="vector"):
        r, w = self._rw([in_], [out])
        o, i = out.ap, in_.ap
        if eng == "scalar":
            self.P.op(eng, lambda e: e.copy(out=o, in_=i), r, w)
        else:
            self.P.op(eng, lambda e: e.tensor_copy(out=o, in_=i), r, w)

    def recip(self, out, in_):
        r, w = self._rw([in_], [out])
        o, i = out.ap, in_.ap
        self.P.op("vector", lambda e: e.reciprocal(out=o, in_=i), r, w)

    def memset(self, out, val, eng="gpsimd"):
        r, w = self._rw([], [out])
        o = out.ap
        self.P.op(eng, lambda e: e.memset(o, val), r, w)

    def mm(self, out, lhsT, rhs, start=True, stop=True):
        r, w = self._rw([lhsT, rhs], [out])
        o, l, x = out.ap, lhsT.ap, rhs.ap
        self.P.op("tensor", lambda e: e.matmul(o, lhsT=l, rhs=x, start=start, stop=stop), r, w)

    def tr(self, out, in_, ident):
        r, w = self._rw([in_, ident], [out])
        o, i, d = out.ap, in_.ap, ident.ap
        self.P.op("tensor", lambda e: e.transpose(o, i, d), r, w)

    def dma(self, out, in_, eng="sync", slow=False):
        r, w = self._rw([in_], [out])
        o, i = out.ap, in_.ap
        if slow:
            self.P.dma(lambda e: e.dma_start(out=o, in_=i, allow_slow_non_contiguous=True), r, w, eng=eng)
        else:
            self.P.dma(lambda e: e.dma_start(out=o, in_=i), r, w, eng=eng)

    def barrier(self, olds, news, dummy):
        w = [b for t in olds for b in t.bufs] + [b for t in news for b in t.bufs] + dummy.bufs
        d = dummy.ap
        self.P.op("gpsimd", lambda e: e.memset(d, 0.0), [], w)


class Tn:
    def __init__(self, ap, shape, nb=1, name="t"):
        self.ap = ap
        self.shape = shape
        self.nb = nb
        self.bufs = [Buf(name + str(i)) for i in range(nb)]

    def __getitem__(self, key):
        if self.nb > 1:
            k1 = key[1] if isinstance(key, tuple) and len(key) > 1 else slice(None)
            if isinstance(k1, int):
                bufs = [self.bufs[k1]]
            elif isinstance(k1, slice):
                bufs = self.bufs[k1]
            else:
                bufs = self.bufs
        else:
            bufs = self.bufs
        return V(self.ap[key], bufs)

    def all(self):
        return V(self.ap, self.bufs)


def dram_in(nc, name, shape, dtype=F32):
    return nc.dram_tensor(name, list(shape), dtype, kind="ExternalInput").ap()


def dram_out(nc, name, shape, dtype=F32):
    return nc.dram_tensor(name, list(shape), dtype, kind="ExternalOutput").ap()


WP = {
    "inh": (8, 512, 8), "ab": (8, 16, 1), "cq": (8, 384, 1), "kv": (8, 384, 1),
    "inga": (8, 512, 2), "ingb": (8, 512, 2),
    "oa": (8, 512, 2), "ob": (8, 512, 2), "out": (8, 512, 2),
    "uq": (3, 512, 4), "ukT": (1, 2048, 1), "uv": (2, 1024, 1),
    "gu": (8, 512, 11), "down": (NFF, 128, 8),
}
WSRC_SHAPE = {
    "inh": (1024, 4096), "ab": (1024, 16), "cq": (1024, 384), "kv": (1024, 384),
    "inga": (1024, 1024), "ingb": (1024, 1024),
    "oa": (1024, 1024), "ob": (1024, 1024), "out": (1024, 1024),
    "uq": (384, 2048), "ukT": (128, 2048), "uv": (256, 1024),
    "gu": (1024, 5632), "down": (DFF, 1024),
}
WSLOT = 4096

PRM_CONV = 0
PRM_QG = 96
PRM_KVG = 99
PRM_LN1G, PRM_LN1B, PRM_LN2G, PRM_LN2B = 101, 109, 117, 125
PRM_GDNG = 133
PRM_ALOG = 261
PRM_DTB = 262
NPRM = 263

CF_ID, CF_ONE, CF_NEG1 = 0, 128, 256
CF_M64 = 384
CF_M32 = 384 + 6 * 128
CF_IND = CF_M32 + 4 * 128
NCF = CF_IND + 2


def _masks(C):
    idx = np.arange(128)
    same = (idx[:, None] // C) == (idx[None, :] // C)
    U = (same & (idx[:, None] <= idx[None, :])).astype(np.float32)
    Cm = same.astype(np.float32)
    NEGT = np.where(same & (idx[None, :] >= idx[:, None]), 0.0, -1.0e4).astype(np.float32)
    MS = (same & (idx[None, :] > idx[:, None])).astype(np.float32)
    return U, Cm - U, NEGT, MS


def host_consts():
    cf = np.zeros((128, NCF), np.float32)
    cf[:, CF_ID:CF_ID + 128] = np.eye(128, dtype=np.float32)
    cf[:, CF_ONE:CF_ONE + 128] = 1.0
    cf[:, CF_NEG1:CF_NEG1 + 128] = -1.0
    U, CmU, NEGT, MS = _masks(64)
    idx = np.arange(128)
    Cc0 = np.repeat((idx < 64).astype(np.float32)[:, None], 128, 1)
    Cc1 = np.repeat((idx >= 64).astype(np.float32)[:, None], 128, 1)
    for k, m in enumerate((U, CmU, NEGT, MS, Cc0, Cc1)):
        cf[:, CF_M64 + k * 128:CF_M64 + (k + 1) * 128] = m
    for k, m in enumerate(_masks(32)):
        cf[:, CF_M32 + k * 128:CF_M32 + (k + 1) * 128] = m
    cf[:, CF_IND] = (idx < 64)
    cf[:, CF_IND + 1] = (idx >= 64)
    oh = np.zeros((8, 8, 128), np.float32)
    for h in range(8):
        oh[h, h, :] = 1.0
    oh = oh.reshape(8, 1024)
    j = np.arange(128)[:, None, None]
    d = np.arange(4)[None, :, None]
    q = np.arange(512)[None, None, :]
    nega = np.where((d * 128 + j) // 64 <= q // 64, 0.0, NEGBIG).astype(np.float32).reshape(128, 2048)
    half = ROPE // 2
    inv = (10000.0 ** (-np.arange(half, dtype=np.float32) / half)).astype(np.float32)

    def tabs(pos):
        ang = pos.astype(np.float32)[None, :] * inv[:, None]
        c, s = np.cos(ang).astype(np.float32), np.sin(ang).astype(np.float32)
        return np.stack([np.concatenate([c, c], 0), np.concatenate([-s, s], 0)], 0)

    ropeT = tabs(np.arange(SEQ))
    rs1 = tabs(PAST + np.arange(SL))
    ropeS = np.concatenate([rs1, rs1], axis=2)
    return {"cst_f": cf, "oh": oh, "nega": nega, "ropeT": np.ascontiguousarray(ropeT),
            "ropeS": np.ascontiguousarray(ropeS)}


def host_weights(w_in, conv_w, a_log, dt_bias, gdn_norm_g, w_oa, q_norm_g, w_uq, kv_norm_g, w_ukv,
                 w_ob, w_out, ln1_g, ln1_b, w_gu, w_down, ln2_g, ln2_b):
    o = {}
    qkv0, z0, a0, b0, cq0, ckv0, kr0, ga0, gb0 = 0, 3072, 4096, 4104, 4112, 4496, 4752, 4816, 5840
    cols = []
    for h in range(H):
        cols += list(range(qkv0 + h * 128, qkv0 + (h + 1) * 128))
        cols += list(range(qkv0 + 1024 + h * 128, qkv0 + 1024 + (h + 1) * 128))
        cols += list(range(qkv0 + 2048 + h * 128, qkv0 + 2048 + (h + 1) * 128))
        cols += list(range(z0 + h * 128, z0 + (h + 1) * 128))
    o["w_inh"] = np.ascontiguousarray(w_in[:, :, cols])
    o["w_ab"] = np.ascontiguousarray(w_in[:, :, a0:a0 + 16])
    o["w_cq"] = np.ascontiguousarray(w_in[:, :, cq0:cq0 + 384])
    krsw = list(range(kr0 + 32, kr0 + 64)) + list(range(kr0, kr0 + 32))
    o["w_kv"] = np.ascontiguousarray(w_in[:, :, list(range(ckv0, ckv0 + 256)) + list(range(kr0, kr0 + 64)) + krsw])
    o["w_inga"] = np.ascontiguousarray(w_in[:, :, ga0:ga0 + 1024])
    o["w_ingb"] = np.ascontiguousarray(w_in[:, :, gb0:gb0 + 1024])
    o["w_oa"], o["w_ob"], o["w_out"] = w_oa, w_ob, w_out
    cols = []
    for h in range(H):
        b = h * 192
        cols += list(range(b, b + 128)) + list(range(b + 128, b + 192))
        cols += list(range(b + 160, b + 192)) + list(range(b + 128, b + 160))
    o["w_uq"] = np.ascontiguousarray(w_uq[:, :, cols])
    wk = w_ukv.reshape(DEPTH, KVR, H, 256)
    o["w_ukT"] = np.ascontiguousarray(np.transpose(wk[:, :, :, :128], (0, 3, 2, 1)).reshape(DEPTH, 128, H * 256))
    o["w_uv"] = np.ascontiguousarray(wk[:, :, :, 128:].reshape(DEPTH, KVR, H * 128))
    cols = []
    for j in range(NFF):
        cols += list(range(j * 128, (j + 1) * 128)) + list(range(DFF + j * 128, DFF + (j + 1) * 128))
    o["w_gu"] = np.ascontiguousarray(w_gu[:, :, cols])
    o["w_down"] = w_down
    prm = np.zeros((DEPTH, 128, NPRM), np.float32)
    for l in range(DEPTH):
        cw = conv_w[l].reshape(4, 24, 128)
        prm[l, :, PRM_CONV:PRM_CONV + 96] = np.transpose(cw, (2, 1, 0)).reshape(128, 96)
        prm[l, :, PRM_QG:PRM_QG + 3] = q_norm_g[l].reshape(3, 128).T
        prm[l, :, PRM_KVG:PRM_KVG + 2] = kv_norm_g[l].reshape(2, 128).T
        prm[l, :, PRM_LN1G:PRM_LN1G + 8] = ln1_g[l].reshape(8, 128).T
        prm[l, :, PRM_LN1B:PRM_LN1B + 8] = ln1_b[l].reshape(8, 128).T
        prm[l, :, PRM_LN2G:PRM_LN2G + 8] = ln2_g[l].reshape(8, 128).T
        prm[l, :, PRM_LN2B:PRM_LN2B + 8] = ln2_b[l].reshape(8, 128).T
        prm[l, :, PRM_GDNG:PRM_GDNG + 128] = gdn_norm_g[l][None, :]
        prm[l, :8, PRM_ALOG] = a_log[l]
        prm[l, :8, PRM_DTB] = dt_bias[l]
    o["prm"] = prm
    return o


class PBank:
    def __init__(self, ap2d, name, dtype_is_bf16=False):
        self.ap = ap2d
        self.bufs = [Buf(name, excl=True)]
        self.bf = dtype_is_bf16

    def full(self, n=128, T=512, c0=0):
        return V(self.ap[:n, c0:c0 + T], self.bufs)

    def q(self, k, n=128, w=128, r0=0):
        return V(self.ap[r0:r0 + n, k * 128:k * 128 + w], self.bufs)


class Rot:
    def __init__(self, items):
        self.items = items
        self.i = 0

    def next(self):
        x = self.items[self.i % len(self.items)]
        self.i += 1
        return x


def build_program(dbg=None):
    nc = bass.Bass("TRN2", target_bir_lowering=False)
    kb = KB(nc)
    P = kb.P
    xT = dram_in(nc, "xT", [D, SEQ])
    xsT = dram_in(nc, "xsT", [D, 2 * SL])
    sconvT = dram_in(nc, "sconvT", [DEPTH, 2, 128, 72])
    sgdn = dram_in(nc, "sgdn", [DEPTH, 2, 128, 1024])
    ckvT = dram_in(nc, "ckvT", [DEPTH, 2, KVR, PAST])
    ckvN = dram_in(nc, "ckvN", [DEPTH, 2, PAST, KVR])
    krT = dram_in(nc, "krT", [DEPTH, 2, ROPE, PAST])
    wsrc = {n: dram_in(nc, "w_" + n, [DEPTH] + list(WSRC_SHAPE[n])) for n in WP}
    prm_d = dram_in(nc, "prm", [DEPTH, 128, NPRM])
    cst_d = dram_in(nc, "cst_f", [128, NCF])
    oh_d = dram_in(nc, "oh", [8, 1024])
    nega_d = dram_in(nc, "nega", [128, 2048])
    ropeT_d = dram_in(nc, "ropeT", [2, ROPE, SEQ])
    ropeS_d = dram_in(nc, "ropeS", [2, ROPE, 2 * SL])

    yT = dram_out(nc, "yT", [D, SEQ])
    ysT = dram_out(nc, "ysT", [D, 2 * SL])
    o_pconv = dram_out(nc, "p_conv", [DEPTH, 128, 72])
    o_pgdn = dram_out(nc, "p_gdn", [DEPTH, 128, 1024])
    o_pckv = dram_out(nc, "p_ckvT", [DEPTH, KVR, SEQ])
    o_pkr = dram_out(nc, "p_krT", [DEPTH, ROPE, SEQ])
    o_sconv = dram_out(nc, "s_conv", [DEPTH, 2, 128, 72])
    o_sgdn = dram_out(nc, "s_gdn", [DEPTH, 2, 128, 1024])
    o_sckv = dram_out(nc, "s_ckvT", [DEPTH, KVR, 2 * SL])
    o_skr = dram_out(nc, "s_krT", [DEPTH, ROPE, 2 * SL])
    x1T = nc.dram_tensor("x1T", [D, SEQ], F32, kind="Internal").ap()
    xs1T = nc.dram_tensor("xs1T", [D, 2 * SL], F32, kind="Internal").ap()
    wbf = {n: nc.dram_tensor("wb_" + n, [DEPTH, WP[n][2], 128, WP[n][0] * WP[n][1]], BF16, kind="Internal").ap()
           for n in WP}
    dram_buf = {k: Buf(k) for k in ("x1T", "xs1T", "out")}
    wbf_buf = {(n, l, p): Buf("wb") for n in WP for l in range(DEPTH) for p in range(WP[n][2])}

    def DV(ap, key="out"):
        return V(ap, [])

    DO = DV
    xblk_buf = {}

    def DI(ap):
        return V(ap, [])

    banks = [PBank(nc.alloc_psum_tensor("pb%d" % i, [128, 512], F32)[:, :], "pb%d" % i) for i in range(7)]
    bbank = PBank(nc.alloc_psum_tensor("pbb", [128, 1024], BF16)[:, :], "pbb", True)
    btr = Rot([(bbank, k) for k in range(8)])

    cf = kb.alloc([128, NCF], F32, name="cf")
    idb = kb.alloc([128, 128], BF16, name="idb")
    oneb = kb.alloc([128, 128], BF16, name="oneb")
    oh = kb.alloc([128, 1024], F32, name="oh")
    prm = kb.alloc([128, DEPTH, NPRM], F32, name="prm")
    nea = kb.alloc([128, DEPTH], F32, name="nea")
    dummy = kb.alloc([128, 8], F32, name="dummy")
    cst = kb.alloc([128, 24, 3], F32, nb=24, name="cst")
    Sf = kb.alloc([128, 8, 128], F32, nb=8, name="Sf")
    Sb = kb.alloc([128, 8, 128], BF16, nb=8, name="Sb")
    KcT = kb.alloc([128, 2, SEQ], BF16, name="KcT")
    KrT = kb.alloc([128, SEQ], BF16, name="KrT")
    Vtok = kb.alloc([128, SEQ // 128, 256], BF16, name="Vtok")
    wslots = [kb.alloc([128, WSLOT], BF16, name="ws%d" % i) for i in range(3)]
    xbf = [kb.alloc([128, 8, TB], BF16, nb=8, name="xbf%d" % i) for i in range(1)]
    mT = kb.alloc([128, 8, TB], BF16, nb=8, name="mT")
    scratch_base = kb.off
    kb.prev_phase = []
    kb.cur_phase = []

    def phase_begin(keep=()):
        kb.prev_phase = [t for t in kb.cur_phase if t not in keep]
        kb.cur_phase = list(keep)
        kb.off = scratch_base
        for t in keep:
            kb.off = max(kb.off, t.end_off)
        kb.new_phase = []

    def sal(shape, dtype, nb=1, name="s"):
        t = kb.alloc(shape, dtype, nb, name)
        t.end_off = kb.off
        kb.cur_phase.append(t)
        kb.new_phase.append(t)
        return t

    def phase_ready():
        kb.barrier(kb.prev_phase, kb.new_phase, dummy.all())

    def cfv(col, n=128, w=128):
        return V(cf.ap[:n, col:col + w], cf.bufs)

    ident_f = lambda n=128: cfv(CF_ID, n, n)
    ident_d = (lambda n=128: V(idb.ap[:n, :n], idb.bufs)) if DBL_DT == BF16 else ident_f
    ones_f = lambda n=128, w=128: cfv(CF_ONE, n, w)
    neg1_f = lambda n=128, w=128: cfv(CF_NEG1, n, w)

    def prmv(l, col, n=128, w=1):
        return V(prm.ap[:n, l, col:col + w], prm.bufs)

    def act_sigmoid(out, x, tmp):
        kb.act(tmp, x, AF.Exp, scale=-1.0)
        kb.act(tmp, tmp, AF.Ln, bias=1.0)
        kb.act(out, tmp, AF.Exp, scale=-1.0)

    def act_rsqrt(out, x, scale, tmp=None):
        t = tmp if tmp is not None else out
        kb.act(t, x, AF.Ln, bias=EPS, scale=scale)
        kb.act(out, t, AF.Exp, scale=-0.5)

    kb.dma(cf.all(), DI(cst_d))
    kb.dma(oh[0:8, :], DI(oh_d))
    kb.dma(prm.all(), DI(prm_d.rearrange("l p n -> p l n")))
    kb.dma(idb.all(), DI(cst_d[:, CF_ID:CF_ID + 128]), eng="gpsimd")
    kb.dma(oneb.all(), DI(cst_d[:, CF_ONE:CF_ONE + 128]), eng="gpsimd")
    for l in range(DEPTH):
        kb.act(nea[0:8, l:l + 1], prmv(l, PRM_ALOG, 8), AF.Exp)
        kb.ts(nea[0:8, l:l + 1], nea[0:8, l:l + 1], -1.0, ALU.mult)
    for l in range(DEPTH):
        for n, (nk, w, npc) in WP.items():
            for p in range(npc):
                src = wsrc[n][l, :, p * w:(p + 1) * w]
                if nk > 1:
                    src = src.rearrange("(k q) w -> q k w", q=128)
                    dst = wbf[n][l, p].rearrange("q (k w) -> q k w", k=nk)
                else:
                    dst = wbf[n][l, p]
                kb.dma(V(dst, wbf_buf[(n, l, p)]), DI(src), eng="gpsimd")

    class WStream:
        AHEAD = 1

        def __init__(self):
            self.seq = []
            self.issued = 0
            self.cons = 0

        def _issue(self, i):
            l, n, p = self.seq[i]
            nk, w, npc = WP[n]
            slot = wslots[i % 3]
            kb.dma(V(slot.ap[:, 0:nk * w], slot.bufs), V(wbf[n][l, p], wbf_buf[(n, l, p)]))

        def get(self, l, n, p=0):
            i = self.cons
            assert self.seq[i] == (l, n, p), (self.seq[i], (l, n, p))
            self.cons += 1
            while self.issued < min(len(self.seq), i + 1 + self.AHEAD):
                self._issue(self.issued)
                self.issued += 1
            nk, w, npc = WP[n]
            slot = wslots[i % 3]

            def view(kc, c0, c1, rows=128):
                return V(slot.ap[:rows, kc * w + c0:kc * w + c1], slot.bufs)
            return view

    wstream = WStream()
    kb.wstream = wstream

    def block_seq(l, phases):
        q = [(l, "ab", 0)] + [(l, "inh", h) for h in range(H)]
        q += [(l, "inga", 0), (l, "oa", 0), (l, "inga", 1), (l, "oa", 1)]
        if phases <= 1:
            return q
        q += [(l, "cq", 0), (l, "kv", 0)] + [(l, "uq", i) for i in range(4)]
        q += [(l, "ingb", 0), (l, "ob", 0), (l, "ingb", 1), (l, "ob", 1)]
        if phases <= 2:
            return q
        q += [(l, "out", 0), (l, "out", 1)]
        if phases <= 3:
            return q
        q += [(l, "gu", i) for i in range(11)] + [(l, "down", i) for i in range(8)]
        return q

    kb.block_seq = block_seq

    class WS:
        def __init__(self, l):
            self.l = l

        def get(self, n, p=0):
            return wstream.get(self.l, n, p)

    def proj(out_v, wv, c0, c1, rhs_fn, nk, rows=128):
        for kc in range(nk):
            kb.mm(out_v, wv(kc, c0, c1, rows), rhs_fn(kc), start=(kc == 0), stop=(kc == nk - 1))

    def block(l, kind, blk, phases=9):
        if kind == "p":
            T, Ls, nseg, n, Cc, L = TB, TB, 1, 128, 64, 5
            tiles = [(i * 128, 128, 0) for i in range(4)]
            xsrc = (xT if l == 0 else x1T)[:, blk * TB:(blk + 1) * TB]
            xdst = (x1T if l == 0 else yT)[:, blk * TB:(blk + 1) * TB]
            xkey_in, xkey_out = ("x1T" if l == 1 else None), ("x1T" if l == 0 else "out")
            MB = CF_M64
            pos0 = blk * TB
        else:
            T, Ls, nseg, n, Cc, L = 2 * SL, SL, 2, 32, 32, 4
            tiles = [(0, 32, 0), (32, 32, 1)]
            xsrc = xsT if l == 0 else xs1T
            xdst = xs1T if l == 0 else ysT
            xkey_in, xkey_out = ("xs1T" if l == 1 else None), ("xs1T" if l == 0 else "out")
            MB = CF_M32
            pos0 = 0
        nch = n // Cc
        ntile = len(tiles)
        xbk = [xblk_buf.setdefault((kind, blk, c), Buf("xblk")) for c in range(8)]
        xin = lambda ap, c=None: V(ap, (xbk if c is None else [xbk[c]])) if l == 1 else DI(ap)
        xout = lambda ap, c: V(ap, [xbk[c]]) if l == 0 else DI(ap)
        U_m, CmU_m, NEGT_m, MS_m = (cfv(MB + k * 128, n, n) for k in range(4))
        xb = xbf[0]
        kb.rr += 1
        ws = WS(l)
        fullrot = Rot([banks[0], banks[1]])
        qrot = Rot([(banks[b], 0) for b in (2, 3, 4)])
        crot = Rot([(banks[b], 0) for b in (5, 6)])
        pfull = lambda rows=128, TT=None: fullrot.next().full(rows, TT or T)

        def pq(rows=128, w=128, r0=0):
            b, k = qrot.next()
            return b.q(k, rows, w, r0)

        def pqc(rows=128, w=128, r0=0):
            b, k = crot.next()
            return b.q(k, rows, w, r0)

        def pbt(rows=128, w=128):
            b, k = btr.next()
            return b.q(k, rows, w)

        xrhs = lambda kc: xb[:, kc, 0:T]
        kb.dma(V(xb.ap[:, :, 0:T], xb.bufs), xin(xsrc.rearrange("(k q) t -> q k t", q=128)), eng="gpsimd")

        phase_begin()
        gT = sal([128, T], F32, name="gT")
        bT = sal([128, T], F32, name="bT")
        gtok = sal([128, ntile, 8], F32, name="gtok")
        btok = sal([128, ntile, 8], F32, name="btok")
        eG = sal([128, ntile, 8], F32, name="eG")
        eGL = sal([128, ntile, 8], F32, name="eGL")
        bexpG = sal([128, ntile, 8], F32, name="bexpG")
        eGLm = sal([128, ntile, 16], F32, name="eGLm")
        glbc = sal([128, ntile * 2, 8], F32, name="glbc")
        oaT = sal([128, 8, T], BF16, nb=8, name="oaT")
        raw = sal([128, nseg * (3 + Ls)], F32, name="raw")
        acc = sal([128, T], F32, name="acc")
        Vb16 = sal([128, T], BF16, name="Vb16")
        sq = sal([128, T], BF16, name="sq")
        rs = sal([128, T], F32, name="rs")
        Qn2 = [sal([128, T], BF16, name="Qn") for _ in range(2)]
        sz2 = [sal([128, T], BF16, name="sz") for _ in range(2)]
        Kn = sal([128, T], BF16, name="Kn")
        Kb = sal([128, T], BF16, name="Kb")
        NTL = ntile
        gU4 = sal([128, NTL, 128], F32, name="gU4")
        Dec4 = sal([128, NTL, 128], F32, name="Dec4")
        NRM = [sal([128, NTL, 384], DBL_DT, name="NRM%d" % i) for i in range(2)]
        Kbg4 = sal([128, NTL, 128], BF16, name="Kbg4")
        tc = [{nm: sal([128, NTL, 128], BF16, name=nm) for nm in ("QKt", "Tt", "kd0", "kd1", "Vbt", "mwT")}
              for hp in range(2)]
        cs_ = []
        for s in range(2):
            d = {"vn": sal([128, 128], BF16, name="vn"), "on": sal([128, 128], BF16, name="on"),
                 "otok": sal([128, 128], F32, name="otok"), "t1": sal([128, 128], F32, name="t1"),
                 "ssq": sal([128, 2], F32, name="ssq")}
            cs_.append(d)
        sg = sal([128, T], F32, name="sg")
        phase_ready()
        for d in cs_:
            kb.memset(d["vn"].all(), 0.0)

        wab = ws.get("ab")
        pa = pfull(8)
        proj(pa, wab, 0, 8, xrhs, 8)
        kb.act(gT[0:8, :], pa, AF.Exp, bias=prmv(l, PRM_DTB, 8))
        kb.act(gT[0:8, :], gT[0:8, :], AF.Ln, bias=1.0)
        kb.ts(gT[0:8, :], gT[0:8, :], V(nea.ap[0:8, l:l + 1], nea.bufs), ALU.mult)
        pb_ = pfull(8)
        proj(pb_, wab, 8, 16, xrhs, 8)
        act_sigmoid(bT[0:8, :], pb_, bT[0:8, :])
        for ti, (t0, nn, sidx) in enumerate(tiles):
            p1 = pq(n, 8)
            kb.tr(p1, gT[0:8, t0:t0 + n], ident_f(8))
            kb.copy(gtok[:n, ti, :], p1, eng="scalar")
            p2 = pq(n, 8)
            kb.tr(p2, bT[0:8, t0:t0 + n], ident_f(8))
            kb.copy(btok[:n, ti, :], p2, eng="vector")
            p3 = pq(n, 8)
            kb.mm(p3, U_m, gtok[:n, ti, :])
            kb.act(eG[:n, ti, :], p3, AF.Exp)
            p4 = pq(n, 8)
            kb.mm(p4, CmU_m, gtok[:n, ti, :])
            kb.act(eGL[:n, ti, :], p4, AF.Exp)
            kb.tt(bexpG[:n, ti, :], btok[:n, ti, :], eG[:n, ti, :], ALU.mult)
            for c in range(nch):
                if nch == 1:
                    kb.copy(eGLm[:n, ti, c * 8:(c + 1) * 8], eGL[:n, ti, :])
                    lhs = ones_f(n, 128)
                else:
                    kb.ts(eGLm[:n, ti, c * 8:(c + 1) * 8], eGL[:n, ti, :], cfv(CF_IND + c, n, 1), ALU.mult)
                    lhs = cfv(CF_M64 + (4 + c) * 128, n, 128)
                p5 = pq(128, 8)
                kb.mm(p5, lhs, gtok[:n, ti, :])
                kb.act(glbc[:, ti * 2 + c, :], p5, AF.Exp)

        if kind == "p" and blk == 0:
            kb.memset(cst.all(), 0.0)
            kb.memset(Sf.all(), 0.0)
            kb.memset(Sb.all(), 0.0)

        def sqv(d, nm):
            return V(d[nm].ap[:n, :n], d[nm].bufs)

        def nr(d, which, half):
            t = d["NR%d" % which]
            return V(t.ap[:n, half * n:(half + 1) * n], t.bufs)

        def pq2(rows, w):
            b_, k_ = qrot.next()
            return V(b_.ap[:rows, 0:w], b_.bufs)

        def rowv(d, nm, rows=None):
            return V(d[nm].ap[:(rows or n), :], d[nm].bufs)

        def gen_AB(h):
            hp = h % 2
            Qn, sz = Qn2[hp], sz2[hp]
            wh = ws.get("inh", h)
            r3 = V(raw.ap.rearrange("p (s c) -> p s c", s=nseg), raw.bufs)
            a3 = V(acc.ap.rearrange("p (s c) -> p s c", s=nseg), acc.bufs)
            for j in range(3):
                ch = j * 8 + h
                pj = pfull()
                proj(pj, wh, j * 128, (j + 1) * 128, xrhs, 8)
                yield
                kb.copy(V(r3.ap[:, :, 3:3 + Ls], r3.bufs), V(pj.ap.rearrange("p (s c) -> p s c", s=nseg), pj.bufs),
                        eng="scalar")
                if kind == "p":
                    kb.copy(V(r3.ap[:, 0, 0:3], r3.bufs), cst[:, ch, :], eng="gpsimd")
                    kb.copy(cst[:, ch, :], V(r3.ap[:, 0, Ls:Ls + 3], r3.bufs), eng="gpsimd")
                    if blk == NBLK - 1:
                        kb.dma(DV(o_pconv[l, :, ch * 3:ch * 3 + 3]), cst[:, ch, :])
                else:
                    for s in range(nseg):
                        kb.dma(V(r3.ap[:, s, 0:3], r3.bufs), DI(sconvT[l, s, :, ch * 3:ch * 3 + 3]))
                        kb.dma(DV(o_sconv[l, s, :, ch * 3:ch * 3 + 3]), V(r3.ap[:, s, Ls:Ls + 3], r3.bufs))
                yield
                kb.ts(a3, V(r3.ap[:, :, 0:Ls], r3.bufs), prmv(l, PRM_CONV + ch * 4 + 0), ALU.mult)
                for jj in range(1, 4):
                    kb.stt(a3, V(r3.ap[:, :, jj:jj + Ls], r3.bufs), prmv(l, PRM_CONV + ch * 4 + jj), a3,
                           ALU.mult, ALU.add)
                    if jj % 2 == 1:
                        yield
                act_sigmoid(rs.all(), acc.all(), rs.all())
                yield
                if j == 2:
                    kb.tt(Vb16.all(), acc.all(), rs.all(), ALU.mult)
                else:
                    kb.tt(acc.all(), acc.all(), rs.all(), ALU.mult)
                    kb.act(sq.all(), acc.all(), AF.Square)
                    pss = pfull()
                    kb.mm(pss, oneb.all(), sq.all())
                    yield
                    act_rsqrt(rs.all(), pss, 1.0)
                    if j == 0:
                        kb.stt(Qn.all(), acc.all(), DK ** -0.5, rs.all(), ALU.mult, ALU.mult)
                    else:
                        kb.tt(Kn.all(), acc.all(), rs.all(), ALU.mult)
                yield
            pz = pfull()
            proj(pz, wh, 384, 512, xrhs, 8)
            yield
            act_sigmoid(rs.all(), pz, rs.all())
            kb.tt(sz.all(), pz, rs.all(), ALU.mult)
            yield
            pbb = pfull()
            kb.mm(pbb, V(oh.ap[0:8, h * 128:(h + 1) * 128], oh.bufs), bT[0:8, :])
            kb.tt(Kb.all(), Kn.all(), pbb, ALU.mult)
            yield

            oN, oR, oM = 0, n, 2 * n
            tcd = tc[hp]

            def bc_mat(v):
                return V(v.ap.unsqueeze(1).to_broadcast([n, NTL, n]), v.bufs)

            def bc_col(t, col, w):
                return V(t.ap[:n, :, col:col + 1].to_broadcast([n, NTL, w]), t.bufs)

            def t4(t, c0, c1, rows=None):
                return V(t.ap[:(rows or n), :, c0:c1], t.bufs)

            def pbank(w, rows=None):
                b_, k_ = qrot.next()
                r_ = rows or n
                return V(b_.ap[:r_, 0:NTL * w].rearrange("p (t c) -> p t c", t=NTL), b_.bufs)

            def pslice(pb, ti, c0=None, c1=None):
                return V(pb.ap[:, ti, :] if c0 is None else pb.ap[:, ti, c0:c1], pb.bufs)

            def tslice(t, ti, c0, c1, rows=None):
                return V(t.ap[:(rows or n), ti, c0:c1], t.bufs)

            tsl = [slice(t0, t0 + n) for (t0, _, _) in tiles]
            kb.tt(t4(gU4, 0, n), bc_mat(U_m), bc_col(gtok, h, n), ALU.mult)
            pb = pbank(n)
            for ti in range(NTL):
                o_ = pslice(pb, ti)
                g_ = tslice(gU4, ti, 0, n)
                kb.mm(o_, ones_f(n, n), g_, start=True, stop=False)
                kb.mm(o_, g_, neg1_f(n, n), start=False, stop=False)
                kb.mm(o_, ident_f(n), NEGT_m, start=False, stop=True)
            kb.act(t4(Dec4, 0, n), pb, AF.Exp)
            yield
            pb = pbank(n)
            for ti in range(NTL):
                kb.mm(pslice(pb, ti), Kn[:, tsl[ti]], Qn[:, tsl[ti]])
            kb.tt(t4(tcd["QKt"], 0, n), pb, t4(Dec4, 0, n), ALU.mult)
            kb.tt(t4(Dec4, 0, n), t4(Dec4, 0, n), bc_mat(MS_m), ALU.mult)
            pb = pbank(n)
            for ti in range(NTL):
                kb.mm(pslice(pb, ti), Kn[:, tsl[ti]], Kb[:, tsl[ti]])
            kb.stt(t4(NRM[0], oN, oN + n), pb, -1.0, t4(Dec4, 0, n), ALU.mult, ALU.mult)
            yield
            pb = pbank(n)
            for ti in range(NTL):
                kb.mm(pslice(pb, ti), tslice(NRM[0], ti, oN, oN + n), ident_d(n))
            kb.copy(t4(NRM[0], oM, oM + n), pb, eng="scalar")
            yield
            pb = pbank(n)
            for ti in range(NTL):
                kb.mm(pslice(pb, ti), tslice(NRM[0], ti, oM, oM + n), tslice(NRM[0], ti, oN, oN + n))
            kb.copy(t4(NRM[1], oN, oN + n), pb, eng="scalar")
            kb.tt(t4(NRM[1], oR, oR + n), t4(NRM[0], oN, oN + n), bc_mat(ident_f(n)), ALU.add)
            pb = pbank(n)
            for ti in range(NTL):
                kb.mm(pslice(pb, ti), tslice(NRM[0], ti, oN, oN + n), tslice(NRM[0], ti, oM, oM + n))
            kb.copy(t4(NRM[1], oM, oM + n), pb, eng="scalar")
            yield
            for k in range(1, L):
                cur, nxt = NRM[k % 2], NRM[(k + 1) % 2]
                if k < L - 1:
                    per = max(1, 512 // (2 * n))
                    for g0 in range(0, NTL, per):
                        g1 = min(NTL, g0 + per)
                        b_, k_ = qrot.next()
                        pg = V(b_.ap[:n, 0:(g1 - g0) * 2 * n].rearrange("p (t c) -> p t c", t=g1 - g0), b_.bufs)
                        for ti in range(g0, g1):
                            kb.mm(V(pg.ap[:, ti - g0, :], pg.bufs), tslice(cur, ti, oM, oM + n),
                                  tslice(cur, ti, oN, oN + 2 * n))
                        kb.copy(V(nxt.ap[:n, g0:g1, oN:oN + n], nxt.bufs), V(pg.ap[:, :, 0:n], pg.bufs), eng="scalar")
                        kb.tt(V(nxt.ap[:n, g0:g1, oR:oR + n], nxt.bufs), V(cur.ap[:n, g0:g1, oR:oR + n], cur.bufs),
                              V(pg.ap[:, :, n:2 * n], pg.bufs), ALU.add)
                else:
                    pb = pbank(n)
                    for ti in range(NTL):
                        kb.mm(pslice(pb, ti), tslice(cur, ti, oM, oM + n), tslice(cur, ti, oR, oR + n))
                    kb.tt(t4(nxt, oR, oR + n), t4(cur, oR, oR + n), pb, ALU.add)
                pb = pbank(n)
                for ti in range(NTL):
                    kb.mm(pslice(pb, ti), tslice(cur, ti, oN, oN + n), tslice(cur, ti, oM, oM + n))
                kb.copy(t4(nxt, oM, oM + n), pb, eng="scalar")
                yield
            cur = NRM[L % 2]
            pb = pbank(n)
            for ti in range(NTL):
                kb.mm(pslice(pb, ti), tslice(cur, ti, oM, oM + n), tslice(cur, ti, oR, oR + n))
            kb.tt(t4(tcd["Tt"], 0, n), t4(cur, oR, oR + n), pb, ALU.add)
            yield
            b_, k_ = btr.next()
            pkt = V(b_.ap[:n, 0:NTL * 128].rearrange("p (t c) -> p t c", t=NTL), b_.bufs)
            for ti in range(NTL):
                kb.tr(V(pkt.ap[:, ti, :], pkt.bufs), Kn[:, tsl[ti]], idb.all())
            kb.tt(t4(Kbg4, 0, 128), pkt, bc_col(bexpG, h, 128), ALU.mult)
            for c in range(nch):
                kb.tt(t4(tcd["kd%d" % c], 0, 128), pkt, bc_col(eGLm, c * 8 + h, 128), ALU.mult)
            b_, k_ = btr.next()
            pvt = V(b_.ap[:n, 0:NTL * 128].rearrange("p (t c) -> p t c", t=NTL), b_.bufs)
            for ti in range(NTL):
                kb.tr(V(pvt.ap[:, ti, :], pvt.bufs), Vb16[:, tsl[ti]], idb.all())
            kb.tt(t4(tcd["Vbt"], 0, 128), pvt, bc_col(btok, h, 128), ALU.mult)
            yield
            pb = pbank(n, rows=128)
            for ti in range(NTL):
                kb.mm(pslice(pb, ti), tslice(Kbg4, ti, 0, 128), tslice(tcd["Tt"], ti, 0, n))
            kb.act(t4(tcd["mwT"], 0, n, rows=128), pb, AF.Copy, scale=-1.0)
            yield

        def gen_C(h):
            hp = h % 2
            Qn, sz = Qn2[hp], sz2[hp]
            for ti, (t0, nn, sidx) in enumerate(tiles):
                e, d = tc[hp], cs_[ti % 2]
                sl = slice(t0, t0 + n)
                if kind == "s":
                    kb.dma(Sf[:, h, :], DI(sgdn[l, sidx, :, h * 128:(h + 1) * 128]))
                    kb.copy(Sb[:, h, :], Sf[:, h, :], eng="gpsimd")
                for c in range(nch):
                    r0 = c * Cc
                    cs = slice(r0, r0 + Cc)
                    pv = pqc(Cc, 128, r0)
                    kb.mm(pv, V(e["Tt"].ap[:n, ti, cs], e["Tt"].bufs), V(e["Vbt"].ap[:n, ti, :], e["Vbt"].bufs), start=True, stop=False)
                    kb.mm(pv, V(e["mwT"].ap[:, ti, cs], e["mwT"].bufs), Sb[:, h, :], start=False, stop=True)
                    po1 = pqc(Cc, 128, r0)
                    kb.mm(po1, Qn[:, t0 + r0:t0 + r0 + Cc], Sb[:, h, :])
                    kb.copy(V(d["vn"].ap[cs, :], d["vn"].bufs), pv, eng="scalar")
                    kb.act(V(d["t1"].ap[cs, :], d["t1"].bufs), po1, AF.Copy, scale=eG[cs, ti, h:h + 1])
                    yield
                    po2 = pqc(Cc, 128, r0)
                    kb.mm(po2, V(e["QKt"].ap[:n, ti, cs], e["QKt"].bufs), rowv(d, "vn"))
                    pS = pqc(128, 128)
                    kb.mm(pS, V(e["kd%d" % c].ap[:n, ti, :], e["kd%d" % c].bufs), rowv(d, "vn"))
                    kb.tt(V(d["otok"].ap[cs, :], d["otok"].bufs), V(d["t1"].ap[cs, :], d["t1"].bufs), po2, ALU.add)
                    kb.stt(Sb[:, h, :], Sf[:, h, :], glbc[:, ti * 2 + c, h:h + 1], pS, ALU.mult, ALU.add)
                    kb.stt(Sf[:, h, :], Sf[:, h, :], glbc[:, ti * 2 + c, h:h + 1], pS, ALU.mult, ALU.add)
                    yield
                if kind == "s":
                    kb.dma(DV(o_sgdn[l, sidx, :, h * 128:(h + 1) * 128]), Sf[:, h, :])
                elif blk == NBLK - 1 and ti == ntile - 1:
                    kb.dma(DV(o_pgdn[l, :, h * 128:(h + 1) * 128]), Sf[:, h, :])
                ot, t1 = rowv(d, "otok"), rowv(d, "t1")
                ssq = V(d["ssq"].ap[:n, 0:1], d["ssq"].bufs)
                kb.act(t1, ot, AF.Square, accum=ssq)
                act_rsqrt(ssq, ssq, 1.0 / 128)
                kb.stt(rowv(d, "on"), ot, ssq, prmv(l, PRM_GDNG, n, 128), ALU.mult, ALU.mult)
                pot = pqc(128, n)
                kb.mm(pot, rowv(d, "on"), V(idb.ap[:n, :n], idb.bufs))
                kb.tt(oaT[:, h, sl], pot, sz[:, sl], ALU.mult)
                yield

        def run_interleaved(gens, weights=None):
            pairs = [(g, (weights[i] if weights else 1)) for i, g in enumerate(gens) if g is not None]
            while pairs:
                for g, w in list(pairs):
                    for _ in range(w):
                        try:
                            next(g)
                        except StopIteration:
                            pairs = [p for p in pairs if p[0] is not g]
                            break

        run_interleaved([gen_AB(0)])
        for h in range(H):
            run_interleaved([gen_C(h), gen_AB(h + 1) if h + 1 < H else None], weights=[1, 2])
        for c in range(8):
            if c % 4 == 0:
                wga = ws.get("inga", c // 4)
                woa = ws.get("oa", c // 4)
            pg = pfull()
            proj(pg, wga, (c % 4) * 128, (c % 4 + 1) * 128, xrhs, 8)
            act_sigmoid(sg.all(), pg, sg.all())
            py = pfull()
            proj(py, woa, (c % 4) * 128, (c % 4 + 1) * 128, lambda kc: oaT[:, kc, :], 8)
            kb.tt(mT[:, c, 0:T], py, sg.all(), ALU.mult)

        if phases <= 1:
            return

        phase_begin()
        CCt = sal([128, T], F32, name="CC")
        SSt = sal([128, T], F32, name="SS")
        if kind == "p":
            NEGA = sal([128, 4, 512], BF16, name="NEGA")
        cqn = sal([128, 3, T], BF16, nb=3, name="cqn")
        obT = sal([128, 8, T], BF16, nb=8, name="obT")
        sg2 = sal([128, T], F32, name="sg2")
        tmpf = sal([128, T], F32, name="tmpf")
        wukb = sal([128, 2048], BF16, name="wukb")
        wuvb = sal([128, 2, 1024], BF16, name="wuvb")
        if kind == "s":
            KnT = sal([128, 2, T], BF16, name="KnT")
            KrnT = sal([128, T], BF16, name="KrnT")
            Vn = sal([128, 2, 256], BF16, name="Vn")
        keep = list(kb.cur_phase)
        cqr = sal([128, 3, T], F32, nb=3, name="cqr")
        ckr = sal([128, 2, T], F32, nb=2, name="ckr")
        sqm = sal([128, 3, T], BF16, nb=3, name="sqm")
        rs2 = sal([128, T], F32, name="rs2")
        kro = sal([128, T], F32, name="kro")
        ktmp = sal([128, T], F32, name="ktmp")
        phase_ready()
        fullrot = Rot([banks[0], banks[1]])
        pfull = lambda rows=128, TT=None: fullrot.next().full(rows, TT or T)
        if kind == "p":
            kb.dma(CCt[0:64, :], DI(ropeT_d[0, :, pos0:pos0 + T]))
            kb.dma(SSt[0:64, :], DI(ropeT_d[1, :, pos0:pos0 + T]))
            kb.dma(V(NEGA.ap.rearrange("p a b -> p (a b)"), NEGA.bufs), DI(nega_d), eng="gpsimd")
        else:
            kb.dma(CCt[0:64, :], DI(ropeS_d[0]))
            kb.dma(SSt[0:64, :], DI(ropeS_d[1]))
        kb.dma(wukb.all(), V(wbf["ukT"][l, 0], wbf_buf[("ukT", l, 0)]))
        kb.dma(V(wuvb.ap.rearrange("p a b -> p (a b)"), wuvb.bufs), V(wbf["uv"][l, 0], wbf_buf[("uv", l, 0)]))
        wcq = ws.get("cq")
        for j in range(3):
            pc = pfull()
            proj(pc, wcq, j * 128, (j + 1) * 128, xrhs, 8)
            kb.copy(cqr[:, j, :], pc, eng="scalar")
            kb.act(sqm[:, j, :], pc, AF.Square)
        pss = pfull()
        for j in range(3):
            kb.mm(pss, oneb.all(), sqm[:, j, :], start=(j == 0), stop=(j == 2))
        kb.act(rs2.all(), pss, AF.Sqrt, bias=EPS, scale=1.0 / QR)
        kb.recip(rs2.all(), rs2.all())
        for j in range(3):
            kb.stt(cqn[:, j, :], cqr[:, j, :], prmv(l, PRM_QG + j), rs2.all(), ALU.mult, ALU.mult)
        wkv = ws.get("kv")
        for j in range(2):
            pc = pfull()
            proj(pc, wkv, j * 128, (j + 1) * 128, xrhs, 8)
            kb.copy(ckr[:, j, :], pc, eng="scalar")
            kb.act(sqm[:, j, :], pc, AF.Square)
        pss = pfull()
        for j in range(2):
            kb.mm(pss, oneb.all(), sqm[:, j, :], start=(j == 0), stop=(j == 1))
        kb.act(rs2.all(), pss, AF.Sqrt, bias=EPS, scale=1.0 / KVR)
        kb.recip(rs2.all(), rs2.all())
        for j in range(2):
            kb.stt(ckr[:, j, :], ckr[:, j, :], prmv(l, PRM_KVG + j), rs2.all(), ALU.mult, ALU.mult)
            if kind == "p":
                kb.dma(DO(o_pckv[l, j * 128:(j + 1) * 128, pos0:pos0 + T]), ckr[:, j, :])
                kb.copy(V(KcT.ap[:, j, pos0:pos0 + T], KcT.bufs), ckr[:, j, :], eng="gpsimd")
            else:
                kb.dma(DO(o_sckv[l, j * 128:(j + 1) * 128, :]), ckr[:, j, :])
                kb.copy(KnT[:, j, :], ckr[:, j, :], eng="gpsimd")
        pA = pfull(64)
        proj(pA, wkv, 256, 320, xrhs, 8)
        pB = pfull(64)
        proj(pB, wkv, 320, 384, xrhs, 8)
        kb.tt(kro[0:64, :], pA, CCt[0:64, :], ALU.mult)
        kb.tt(ktmp[0:64, :], pB, SSt[0:64, :], ALU.mult)
        kb.tt(kro[0:64, :], kro[0:64, :], ktmp[0:64, :], ALU.add, eng="gpsimd")
        if kind == "p":
            kb.dma(DO(o_pkr[l, :, pos0:pos0 + T]), kro[0:64, :])
            kb.copy(V(KrT.ap[0:64, pos0:pos0 + T], KrT.bufs), kro[0:64, :], eng="gpsimd")
            for t in range(4):
                for j in range(2):
                    pt_ = pbt(128, 128)
                    kb.tr(pt_, V(KcT.ap[:, j, pos0 + t * 128:pos0 + (t + 1) * 128], KcT.bufs), idb.all())
                    kb.copy(V(Vtok.ap[:, blk * 4 + t, j * 128:(j + 1) * 128], Vtok.bufs), pt_,
                            eng=("scalar" if j == 0 else "vector"))
        else:
            kb.dma(DO(o_skr[l]), kro[0:64, :])
            kb.copy(KrnT[0:64, :], kro[0:64, :], eng="gpsimd")
            for s in range(2):
                for j in range(2):
                    pt_ = pbt(32, 128)
                    kb.tr(pt_, KnT[:, j, s * 32:(s + 1) * 32], idb.all())
                    kb.copy(V(Vn.ap[0:32, s, j * 128:(j + 1) * 128], Vn.bufs), pt_,
                            eng=("scalar" if j == 0 else "vector"))

        phase_begin(keep=keep)
        NQ = 2
        hd = []
        for s in range(NQ):
            hd.append({"qn": sal([128, T], BF16, name="qn"), "qr": sal([128, T], F32, name="qr"),
                       "qt": sal([128, T], F32, name="qt"), "qrb": sal([128, T], BF16, name="qrb"),
                       "qlat": sal([128, 2, T], BF16, name="qlat")})
        rden = sal([128, T if kind == "p" else 256], F32, name="rden")
        if kind == "p":
            Pacc = sal([128, T], F32, name="Pacc")
            Pab = sal([128, T], BF16, name="Pab")
        if kind == "p":
            PT = [sal([128, T], BF16, name="PT") for _ in range(4)]
            olat = [sal([128, 2, T], BF16, name="olat") for _ in range(2)]
        else:
            PT = [sal([128, 256], BF16, name="PT") for _ in range(3)]
            qlS = sal([128, 4, 256], BF16, name="qlS")
            qrS = sal([128, 2, 256], BF16, name="qrS")
            olS = sal([128, 2, 256], BF16, name="olS")
            spb = [(sal([128, 2, 1024], BF16, name="Ksp"), sal([128, 1024], BF16, name="Krp"),
                    sal([128, 8, 256], BF16, name="Vsp")) for _ in range(2)]
        phase_ready()
        srot = Rot([banks[2], banks[3], banks[4]] if kind == "p" else [banks[3], banks[4]])
        pO0, pO1, pD = banks[5], banks[6], (banks[0] if kind == "p" else banks[2])
        wuk = lambda c0, c1: V(wukb.ap[:, c0:c1], wukb.bufs)
        wuvv = lambda j, c0, c1: V(wuvb.ap[:, j, c0:c1], wuvb.bufs)
        crhs = lambda kc: cqn[:, kc, :]
        for h in range(H):
            d = hd[h % NQ]
            if h % 2 == 0:
                wuq = ws.get("uq", h // 2)
            b0 = (h % 2) * 256
            pn = pfull()
            proj(pn, wuq, b0, b0 + 128, crhs, 3)
            kb.copy(d["qn"].all(), pn, eng="scalar")
            pA = pfull(64)
            proj(pA, wuq, b0 + 128, b0 + 192, crhs, 3)
            pB = pfull(64)
            proj(pB, wuq, b0 + 192, b0 + 256, crhs, 3)
            kb.tt(d["qr"][0:64, :], pA, CCt[0:64, :], ALU.mult)
            kb.tt(d["qt"][0:64, :], pB, SSt[0:64, :], ALU.mult)
            if kind == "p":
                kb.tt(d["qrb"][0:64, :], d["qr"][0:64, :], d["qt"][0:64, :], ALU.add, eng="gpsimd")
            else:
                for s in range(2):
                    kb.tt(qrS[0:64, s, h * 32:(h + 1) * 32], d["qr"][0:64, s * 32:(s + 1) * 32],
                          d["qt"][0:64, s * 32:(s + 1) * 32], ALU.add, eng="gpsimd")
            for j in range(2):
                pl = pfull()
                kb.mm(pl, wuk(h * 256 + j * 128, h * 256 + (j + 1) * 128), d["qn"].all())
                if kind == "p":
                    kb.copy(d["qlat"][:, j, :], pl, eng=("scalar" if j == 0 else "vector"))
                else:
                    for s in range(2):
                        kb.copy(qlS[:, s * 2 + j, h * 32:(h + 1) * 32],
                                V(pl.ap[:, s * 32:(s + 1) * 32], pl.bufs), eng=("scalar" if s == 0 else "vector"))
            if kind != "p":
                continue
            nkt = blk * 4 + 4
            ol = olat[h % 2]

            def qk(kt):
                dgn = kt - blk * 4
                q0 = dgn * 128 if dgn > 0 else 0
                pS = srot.next().full(128, T - q0, q0)
                ks = slice(kt * 128, (kt + 1) * 128)
                kb.mm(pS, V(KcT.ap[:, 0, ks], KcT.bufs), d["qlat"][:, 0, q0:T], start=True, stop=False)
                kb.mm(pS, V(KcT.ap[:, 1, ks], KcT.bufs), d["qlat"][:, 1, q0:T], start=False, stop=False)
                kb.mm(pS, V(KrT.ap[0:64, ks], KrT.bufs), d["qrb"][0:64, q0:T], start=False, stop=(dgn < 0))
                if dgn >= 0:
                    kb.mm(pS, idb.all(), NEGA[:, dgn, q0:T], start=False, stop=True)
                pt = PT[kt % 4]
                kb.act(pt[:, q0:T], pS, AF.Exp, scale=SCALE)
                return pt, q0

            def pv(kt, pt, q0):
                st, sp = (kt == 0), (kt == nkt - 1)
                kb.mm(pO0.full(128, T - q0, q0), V(Vtok.ap[:, kt, 0:128], Vtok.bufs), pt[:, q0:T], start=st, stop=sp)
                kb.mm(pO1.full(128, T - q0, q0), V(Vtok.ap[:, kt, 128:256], Vtok.bufs), pt[:, q0:T], start=st, stop=sp)
                if kt == 0:
                    kb.copy(Pacc[:, q0:T], pt[:, q0:T])
                else:
                    kb.tt(Pacc[:, q0:T], Pacc[:, q0:T], pt[:, q0:T], ALU.add)

            LA = 2
            pend = {}
            for kt in range(nkt + LA):
                if kt < nkt:
                    pend[kt] = qk(kt)
                if kt - LA >= 0:
                    pv(kt - LA, *pend.pop(kt - LA))
            kb.copy(Pab.all(), Pacc.all(), eng="scalar")
            kb.mm(pD.full(128, T), oneb.all(), Pab.all())
            kb.recip(rden.all(), pD.full(128, T))
            kb.tt(ol[:, 0, :], pO0.full(128, T), rden.all(), ALU.mult)
            kb.tt(ol[:, 1, :], pO1.full(128, T), rden.all(), ALU.mult)
            pob = pfull()
            for j in range(2):
                kb.mm(pob, wuvv(j, h * 128, (h + 1) * 128), ol[:, j, :], start=(j == 0), stop=(j == 1))
            kb.copy(obT[:, h, :], pob, eng="scalar")
        if kind == "s":
            for s in range(2):
                items = []
                for pc in range(PAST // 1024):
                    Ksp, Krp, Vsp = spb[pc % 2]
                    cs = slice(pc * 1024, (pc + 1) * 1024)
                    for j in range(2):
                        kb.dma(Ksp[:, j, :], DI(ckvT[l, s, j * 128:(j + 1) * 128, cs]), eng="gpsimd")
                    kb.dma(Krp[0:64, :], DI(krT[l, s, :, cs]), eng="gpsimd")
                    kb.dma(Vsp.all(), DI(ckvN[l, s, cs, :].rearrange("(t q) c -> q t c", q=128)), eng="gpsimd")
                    for t in range(8):
                        ts_ = slice(t * 128, (t + 1) * 128)
                        items.append((Ksp[:, 0, ts_], Ksp[:, 1, ts_], Krp[0:64, ts_], Vsp[:, t, 0:128],
                                      Vsp[:, t, 128:256], 128))
                ss = slice(s * 32, (s + 1) * 32)
                items.append((KnT[:, 0, ss], KnT[:, 1, ss], KrnT[0:64, ss], Vn[0:32, s, 0:128], Vn[0:32, s, 128:256], 32))
                nit = len(items)
                for i, (k0, k1, kr_, v0, v1, rows) in enumerate(items):
                    pS = srot.next().full(rows, 256)
                    kb.mm(pS, k0, qlS[:, s * 2 + 0, :], start=True, stop=False)
                    kb.mm(pS, k1, qlS[:, s * 2 + 1, :], start=False, stop=False)
                    kb.mm(pS, kr_, qrS[0:64, s, :], start=False, stop=True)
                    pt = PT[i % 3]
                    kb.act(pt[0:rows, :], pS, AF.Exp, scale=SCALE)
                    st, sp = (i == 0), (i == nit - 1)
                    kb.mm(pO0.full(128, 256), v0, pt[0:rows, :], start=st, stop=sp)
                    kb.mm(pO1.full(128, 256), v1, pt[0:rows, :], start=st, stop=sp)
                    kb.mm(pD.full(128, 256), V(oneb.ap[0:rows, :], oneb.bufs), pt[0:rows, :], start=st, stop=sp)
                kb.recip(rden.all(), pD.full(128, 256))
                kb.tt(olS[:, 0, :], pO0.full(128, 256), rden.all(), ALU.mult)
                kb.tt(olS[:, 1, :], pO1.full(128, 256), rden.all(), ALU.mult)
                for h in range(H):
                    pob = pfull(128, 32)
                    for j in range(2):
                        kb.mm(pob, wuvv(j, h * 128, (h + 1) * 128), olS[:, j, h * 32:(h + 1) * 32],
                              start=(j == 0), stop=(j == 1))
                    kb.copy(obT[:, h, s * 32:(s + 1) * 32], pob, eng="scalar")
        for c in range(8):
            if c % 4 == 0:
                wgb = ws.get("ingb", c // 4)
                wob = ws.get("ob", c // 4)
            pg = pfull()
            proj(pg, wgb, (c % 4) * 128, (c % 4 + 1) * 128, xrhs, 8)
            kb.act(sg2.all(), pg, AF.Sigmoid)
            py = pfull()
            proj(py, wob, (c % 4) * 128, (c % 4 + 1) * 128, lambda kc: obT[:, kc, :], 8)
            kb.tt(tmpf.all(), py, sg2.all(), ALU.mult)
            kb.tt(mT[:, c, 0:T], mT[:, c, 0:T], tmpf.all(), ALU.add, eng="gpsimd")
        if phases <= 2:
            return

        def ln_tmps():
            return ([sal([128, T], BF16, name="ub") for _ in range(2)], [sal([128, T], BF16, name="sqc") for _ in range(2)],
                    sal([128, T], F32, name="mean"), sal([128, T], F32, name="rstd"))

        def layer_norm(u, tm, pm, pvb, gcol, bcol, out_bf, out_dram):
            ub, sqc, mean, rstd = tm
            kb.act(mean.all(), pm, AF.Copy, scale=1.0 / D)
            for c in range(8):
                kb.tt(u[:, c, :], u[:, c, :], mean.all(), ALU.subtract)
                kb.act(sqc[c % 2].all(), u[:, c, :], AF.Square)
                kb.mm(pvb, oneb.all(), sqc[c % 2].all(), start=(c == 0), stop=(c == 7))
            kb.act(rstd.all(), pvb, AF.Sqrt, bias=EPS, scale=1.0 / D)
            kb.recip(rstd.all(), rstd.all())
            for c in range(8):
                kb.tt(u[:, c, :], u[:, c, :], rstd.all(), ALU.mult)
                kb.act(u[:, c, :], u[:, c, :], AF.Identity, scale=prmv(l, gcol + c), bias=prmv(l, bcol + c))
                if out_bf is not None:
                    kb.copy(out_bf[:, c, 0:T], u[:, c, :], eng="gpsimd")
                if out_dram is not None:
                    kb.dma(xout(xdst[c * 128:(c + 1) * 128, :], c), u[:, c, :])

        phase_begin()
        u = sal([128, 8, T], F32, nb=8, name="u")
        xr = [sal([128, T], F32, name="xr") for _ in range(2)]
        tm = ln_tmps()
        phase_ready()
        fullrot = Rot([banks[0], banks[1], banks[2], banks[3], banks[4]])
        pfull = lambda rows=128, TT=None: fullrot.next().full(rows, TT or T)
        pm, pvb = banks[5].full(128, T), banks[6].full(128, T)
        for c in range(8):
            if c % 4 == 0:
                wo = ws.get("out", c // 4)
            po = pfull()
            proj(po, wo, (c % 4) * 128, (c % 4 + 1) * 128, lambda kc: mT[:, kc, 0:T], 8)
            kb.dma(xr[c % 2].all(), xin(xsrc[c * 128:(c + 1) * 128, :], c))
            kb.stt(u[:, c, :], xr[c % 2].all(), ALPHA, po, ALU.mult, ALU.add)
            kb.copy(tm[0][c % 2].all(), u[:, c, :], eng="scalar")
            kb.mm(pm, oneb.all(), tm[0][c % 2].all(), start=(c == 0), stop=(c == 7))
        layer_norm(u, tm, pm, pvb, PRM_LN1G, PRM_LN1B, xb, None)
        if phases <= 3:
            for c in range(8):
                kb.dma(xout(xdst[c * 128:(c + 1) * 128, :], c), u[:, c, :])
            return

        phase_begin(keep=[u])
        hid = sal([128, NFF, T], BF16, nb=NFF, name="hid")
        s1 = [sal([128, T], F32, name="s1") for _ in range(2)]
        tm = ln_tmps()
        phase_ready()
        x1rhs = lambda kc: xb[:, kc, 0:T]
        for j in range(NFF):
            if j % 2 == 0:
                wg = ws.get("gu", j // 2)
            b0 = (j % 2) * 256
            p1 = pfull()
            proj(p1, wg, b0, b0 + 128, x1rhs, 8)
            p3 = pfull()
            proj(p3, wg, b0 + 128, b0 + 256, x1rhs, 8)
            kb.act(s1[j % 2].all(), p1, AF.Silu)
            kb.tt(hid[:, j, :], p3, s1[j % 2].all(), ALU.mult)
        for c in range(8):
            wd = ws.get("down", c)
            po = pfull()
            proj(po, wd, 0, 128, lambda kc: hid[:, kc, :], NFF)
            kb.stt(u[:, c, :], u[:, c, :], ALPHA, po, ALU.mult, ALU.add)
            kb.copy(tm[0][c % 2].all(), u[:, c, :], eng="scalar")
            kb.mm(pm, oneb.all(), tm[0][c % 2].all(), start=(c == 0), stop=(c == 7))
        layer_norm(u, tm, pm, pvb, PRM_LN2G, PRM_LN2B, None, True)

    kb.block = block
    return nc, kb


_CACHE = {}


def _full_plan():
    plan = []
    for l in range(DEPTH):
        plan.append((l, "s", 0))
        for b in range(NBLK):
            plan.append((l, "p", b))
    return plan


def get_program(plan=None, phases=9):
    key = (tuple(plan) if plan else None, phases)
    if key not in _CACHE:
        nc, kb = build_program()
        for (l, kind, blk) in (plan or _full_plan()):
            kb.wstream.seq += kb.block_seq(l, phases)
        for (l, kind, blk) in (plan or _full_plan()):
            kb.block(l, kind, blk, phases)
        kb.P.finish()
        kb.P.emit()
        _CACHE[key] = (nc, kb)
    return _CACHE[key]


def kernel(x_prompt, x_sample, state_conv, state_gdn, cache_ckv, cache_krope, w_in, conv_w, a_log,
           dt_bias, gdn_norm_g, w_oa, q_norm_g, w_uq, kv_norm_g, w_ukv, w_ob, w_out, ln1_g, ln1_b,
           w_gu, w_down, ln2_g, ln2_b, _plan=None, _phases=9, _ncores=8):
    f = lambda a: np.ascontiguousarray(np.asarray(a, dtype=np.float32))
    x_prompt, x_sample, state_conv, state_gdn, cache_ckv, cache_krope = map(
        f, (x_prompt, x_sample, state_conv, state_gdn, cache_ckv, cache_krope))
    hw = host_weights(*map(f, (w_in, conv_w, a_log, dt_bias, gdn_norm_g, w_oa, q_norm_g, w_uq, kv_norm_g,
                               w_ukv, w_ob, w_out, ln1_g, ln1_b, w_gu, w_down, ln2_g, ln2_b)))
    hc = host_consts()
    nc, kb = get_program(_plan, _phases)
    ncores = _ncores
    in_maps = []
    for c in range(ncores):
        m = {}
        m["xT"] = np.ascontiguousarray(x_prompt[c % NSEQ].T)
        sb = [2 * c, 2 * c + 1]
        m["xsT"] = np.ascontiguousarray(x_sample[sb].reshape(2 * SL, D).T)
        sc = state_conv[:, sb]
        m["sconvT"] = np.ascontiguousarray(
            np.transpose(sc.reshape(DEPTH, 2, 3, 24, 128), (0, 1, 4, 3, 2)).reshape(DEPTH, 2, 128, 72))
        sg = state_gdn[:, sb]
        m["sgdn"] = np.ascontiguousarray(np.transpose(sg, (0, 1, 3, 2, 4)).reshape(DEPTH, 2, 128, 1024))
        ck = cache_ckv[:, sb]
        m["ckvN"] = np.ascontiguousarray(ck)
        m["ckvT"] = np.ascontiguousarray(np.transpose(ck, (0, 1, 3, 2)))
        m["krT"] = np.ascontiguousarray(np.transpose(cache_krope[:, sb], (0, 1, 3, 2)))
        for n in WP:
            m["w_" + n] = hw["w_" + n]
        m["prm"] = hw["prm"]
        m.update(hc)
        in_maps.append(m)
    res = run_bass_kernel_spmd(nc, in_maps, core_ids=list(range(ncores)))
    R = list(res.results)
    while len(R) < 8:
        R.append(R[0])
    ncores = 8
    yp = np.stack([R[c]["yT"].T for c in range(NSEQ)], 0)
    ys = np.concatenate([R[c]["ysT"].T.reshape(2, SL, D) for c in range(ncores)], 0)

    def conv_back(a):
        sh = a.shape[:-2]
        a = a.reshape(sh + (128, 24, 3))
        nd = len(sh)
        a = np.transpose(a, tuple(range(nd)) + (nd + 2, nd + 1, nd))
        return np.ascontiguousarray(a.reshape(sh + (3, 3072)))

    def gdn_back(a):
        sh = a.shape[:-2]
        a = a.reshape(sh + (128, 8, 128))
        nd = len(sh)
        return np.ascontiguousarray(np.transpose(a, tuple(range(nd)) + (nd + 1, nd, nd + 2)))

    p_conv = np.stack([conv_back(R[c]["p_conv"]) for c in range(NSEQ)], 1)
    p_gdn = np.stack([gdn_back(R[c]["p_gdn"]) for c in range(NSEQ)], 1)
    p_ckv = np.stack([np.transpose(R[c]["p_ckvT"], (0, 2, 1)) for c in range(NSEQ)], 1)
    p_kr = np.stack([np.transpose(R[c]["p_krT"], (0, 2, 1)) for c in range(NSEQ)], 1)
    s_conv = np.concatenate([conv_back(R[c]["s_conv"]) for c in range(ncores)], 1)
    s_gdn = np.concatenate([gdn_back(R[c]["s_gdn"]) for c in range(ncores)], 1)
    s_ckv = np.concatenate([np.transpose(R[c]["s_ckvT"], (0, 2, 1)).reshape(DEPTH, 2, SL, KVR)
                            for c in range(ncores)], 1)
    s_kr = np.concatenate([np.transpose(R[c]["s_krT"], (0, 2, 1)).reshape(DEPTH, 2, SL, ROPE)
                           for c in range(ncores)], 1)
    outs = (yp, ys, p_conv, p_gdn, p_ckv, p_kr, s_conv, s_gdn, s_ckv, s_kr)
    return tuple(np.ascontiguousarray(o, dtype=np.float32) for o in outs)
```

```python
import contextlib
import numpy as np
import concourse.bass as bass
import concourse.mybir as mybir
from concourse.bass_utils import run_bass_kernel_spmd

F32 = mybir.dt.float32
BF16 = mybir.dt.bfloat16
U8 = mybir.dt.uint8
AF = mybir.ActivationFunctionType
ALU = mybir.AluOpType

ENGS = ("tensor", "vector", "scalar", "gpsimd", "sync")

D = 1024
SEQ = 8192
NSEQ = 4
DEPTH = 2
SB_, SL = 16, 32
PAST = 4096
H = 8
DK = 128
CQKV = 3072
QR, KVR, ROPE, NOPE, VB = 384, 256, 64, 128, 128
DFF = 2816
NFF = DFF // 128
ALPHA = (2 * DEPTH) ** 0.25
EPS = 1e-6
SCALE = (NOPE + ROPE) ** -0.5
TB = 512
NBLK = SEQ // TB
NEGBIG = -30000.0
DBL_DT = BF16


class Buf:
    __slots__ = ("name", "last_w", "readers", "excl")

    def __init__(self, name="", excl=False):
        self.name = name
        self.last_w = None
        self.readers = []
        self.excl = excl


class V:
    __slots__ = ("ap", "bufs")

    def __init__(self, ap, bufs):
        self.ap = ap
        self.bufs = bufs if isinstance(bufs, (list, tuple)) else [bufs]


class Prog:
    N_DMA_SEMS = 24

    def __init__(self, nc):
        self.nc = nc
        self.ops = {e: [] for e in ENGS}
        self.sem_names = list(ENGS) + ["dma%d" % i for i in range(self.N_DMA_SEMS)]
        self.sem_idx = {n: i for i, n in enumerate(self.sem_names)}
        self.sem_tot = [0] * len(self.sem_names)
        self.known = {e: [0] * len(self.sem_names) for e in ENGS}
        self.dma_rrs = {}
        self.n_ops = 0
        self.n_waits = 0

    def _add(self, eng, fn, reads, writes, semidx, inc):
        ns = len(self.sem_names)
        need = [0] * ns
        known = self.known[eng]
        pe_own = self.sem_idx["tensor"] if eng == "tensor" else -1
        ents = []
        for b in reads:
            if b.last_w is not None:
                ents.append(b.last_w)
        for b in writes:
            if b.last_w is not None:
                ents.append(b.last_w)
            ents.extend(b.readers)
        for (s, t, vc) in ents:
            if s == pe_own:
                continue
            if t > need[s]:
                need[s] = t
        waits = [(s, need[s]) for s in range(ns) if need[s] > known[s]]
        for (s, t, vc) in ents:
            if s == pe_own:
                continue
            for i in range(ns):
                if vc[i] > known[i]:
                    known[i] = vc[i]
        for s, t in waits:
            if t > known[s]:
                known[s] = t
        self.sem_tot[semidx] += inc
        ticket = self.sem_tot[semidx]
        vc = list(known)
        if ticket > vc[semidx]:
            vc[semidx] = ticket
        ent = (semidx, ticket, vc)
        wset = set(id(b) for b in writes)
        for b in writes:
            b.last_w = ent
            b.readers = []
        for b in reads:
            if id(b) not in wset:
                b.readers.append(ent)
                if len(b.readers) > 12:
                    latest = {}
                    for r in b.readers:
                        if r[0] not in latest or latest[r[0]][1] < r[1]:
                            latest[r[0]] = r
                    b.readers = list(latest.values())
        self.ops[eng].append((waits, fn, semidx, inc))
        self.n_ops += 1
        self.n_waits += len(waits)

    def op(self, eng, fn, reads=(), writes=()):
        self._add(eng, fn, tuple(reads), tuple(writes), self.sem_idx[eng], 1)

    def dma(self, fn, reads=(), writes=(), eng="sync"):
        lo, hi = (0, 16) if eng == "sync" else (16, self.N_DMA_SEMS)
        k = lo + self.dma_rrs.get(eng, 0)
        self.dma_rrs[eng] = (self.dma_rrs.get(eng, 0) + 1) % (hi - lo)
        semidx = self.sem_idx["dma%d" % k]
        prev = self.sem_tot[semidx]
        if prev > self.known[eng][semidx]:
            self.ops[eng].append(([(semidx, prev)], None, None, 0))
            self.known[eng][semidx] = prev
            self.n_waits += 1
        self._add(eng, fn, tuple(reads), tuple(writes), semidx, 16)

    def finish(self):
        for eng in ENGS:
            waits = []
            for s in range(len(self.sem_names)):
                if self.sem_tot[s] > self.known[eng][s]:
                    waits.append((s, self.sem_tot[s]))
            self.ops[eng].append((waits, None, None, 0))

    def emit(self):
        nc = self.nc
        with contextlib.ExitStack() as st:
            sems = [st.enter_context(nc.semaphore("s_" + n)) for n in self.sem_names]
            block = st.enter_context(nc.Block())

            def run(engname):
                def body(eng):
                    for waits, fn, semidx, inc in self.ops[engname]:
                        for s, t in waits:
                            eng.wait_ge(sems[s], t)
                        if fn is not None:
                            fn(eng).then_inc(sems[semidx], inc)
                return body

            block.tensor(run("tensor"))
            block.vector(run("vector"))
            block.scalar(run("scalar"))
            block.gpsimd(run("gpsimd"))
            block.sync(run("sync"))


class KB:
    def __init__(self, nc):
        self.nc = nc
        self.P = Prog(nc)
        self.cap = 204 * 1024
        self.arena = nc.alloc_sbuf_tensor("arena", [128, self.cap], U8)
        self.off = 0
        self.rr = 0

    def alloc(self, shape, dtype, nb=1, name="t"):
        esz = 4 if dtype == F32 else 2
        n = int(np.prod(shape[1:]))
        nbytes = (n * esz + 31) // 32 * 32
        assert self.off + nbytes <= self.cap, ("SBUF overflow", name, self.off, nbytes)
        ap = self.arena[:, self.off:self.off + n * esz].bitcast(dtype)
        self.off += nbytes
        if len(shape) == 3:
            ap = ap.rearrange("p (a b) -> p a b", a=shape[1])
        elif len(shape) == 4:
            ap = ap.rearrange("p (a b c) -> p a b c", a=shape[1], b=shape[2])
        return Tn(ap, shape, nb, name)

    def _rw(self, ins, outs):
        r, w = [], []
        for v in ins:
            if isinstance(v, V):
                for b in v.bufs:
                    (w if b.excl else r).append(b)
        for v in outs:
            w.extend(v.bufs)
        return r, w

    def act(self, out, in_, func, bias=None, scale=None, accum=None):
        kw = {}
        ins = [in_]
        if bias is not None:
            kw["bias"] = bias.ap if isinstance(bias, V) else bias
            ins.append(bias)
        if scale is not None:
            kw["scale"] = scale.ap if isinstance(scale, V) else scale
            ins.append(scale)
        outs = [out]
        if accum is not None:
            kw["accum_out"] = accum.ap
            outs.append(accum)
        r, w = self._rw(ins, outs)
        o, i = out.ap, in_.ap
        self.P.op("scalar", lambda e: e.activation(out=o, in_=i, func=func, **kw), r, w)

    def tt(self, out, a, b, op, eng="vector"):
        r, w = self._rw([a, b], [out])
        o, x, y = out.ap, a.ap, b.ap
        self.P.op(eng, lambda e: e.tensor_tensor(out=o, in0=x, in1=y, op=op), r, w)

    def ts(self, out, a, s1, op0, s2=None, op1=None, eng="vector"):
        r, w = self._rw([a, s1, s2], [out])
        o, x = out.ap, a.ap
        c1 = s1.ap if isinstance(s1, V) else s1
        c2 = s2.ap if isinstance(s2, V) else s2
        if op1 is None:
            self.P.op(eng, lambda e: e.tensor_scalar(out=o, in0=x, scalar1=c1, scalar2=None, op0=op0), r, w)
        else:
            self.P.op(eng, lambda e: e.tensor_scalar(out=o, in0=x, scalar1=c1, scalar2=c2, op0=op0, op1=op1), r, w)

    def stt(self, out, a, s, b, op0, op1, eng="vector"):
        r, w = self._rw([a, s, b], [out])
        o, x, y = out.ap, a.ap, b.ap
        c = s.ap if isinstance(s, V) else s
        self.P.op(eng, lambda e: e.scalar_tensor_tensor(out=o, in0=x, scalar=c, in1=y, op0=op0, op1=op1), r, w)

    def copy(self, out, in_, eng="vector"):
        r, w = self._rw([in_], [out])
        o, i = out.ap, in_.ap
        if eng == "scalar":
            self.P.op(eng, lambda e: e.copy(out=o, in_=i), r, w)
        else:
            self.P.op(eng, lambda e: e.tensor_copy(out=o, in_=i), r, w)

    def recip(self, out, in_):
        r, w = self._rw([in_], [out])
        o, i = out.ap, in_.ap
        self.P.op("vector", lambda e: e.reciprocal(out=o, in_=i), r, w)

    def memset(self, out, val, eng="gpsimd"):
        r, w = self._rw([], [out])
        o = out.ap
        self.P.op(eng, lambda e: e.memset(o, val), r, w)

    def mm(self, out, lhsT, rhs, start=True, stop=True):
        r, w = self._rw([lhsT, rhs], [out])
        o, l, x = out.ap, lhsT.ap, rhs.ap
        self.P.op("tensor", lambda e: e.matmul(o, lhsT=l, rhs=x, start=start, stop=stop), r, w)

    def tr(self, out, in_, ident):
        r, w = self._rw([in_, ident], [out])
        o, i, d = out.ap, in_.ap, ident.ap
        self.P.op("tensor", lambda e: e.transpose(o, i, d), r, w)

    def dma(self, out, in_, eng="sync", slow=False):
        r, w = self._rw([in_], [out])
        o, i = out.ap, in_.ap
        if slow:
            self.P.dma(lambda e: e.dma_start(out=o, in_=i, allow_slow_non_contiguous=True), r, w, eng=eng)
        else:
            self.P.dma(lambda e: e.dma_start(out=o, in_=i), r, w, eng=eng)

    def barrier(self, olds, news, dummy):
        w = [b for t in olds for b in t.bufs] + [b for t in news for b in t.bufs] + dummy.bufs
        d = dummy.ap
        self.P.op("gpsimd", lambda e: e.memset(d, 0.0), [], w)


class Tn:
    def __init__(self, ap, shape, nb=1, name="t"):
        self.ap = ap
        self.shape = shape
        self.nb = nb
        self.bufs = [Buf(name + str(i)) for i in range(nb)]

    def __getitem__(self, key):
        if self.nb > 1:
            k1 = key[1] if isinstance(key, tuple) and len(key) > 1 else slice(None)
            if isinstance(k1, int):
                bufs = [self.bufs[k1]]
            elif isinstance(k1, slice):
                bufs = self.bufs[k1]
            else:
                bufs = self.bufs
        else:
            bufs = self.bufs
        return V(self.ap[key], bufs)

    def all(self):
        return V(self.ap, self.bufs)


def dram_in(nc, name, shape, dtype=F32):
    return nc.dram_tensor(name, list(shape), dtype, kind="ExternalInput").ap()


def dram_out(nc, name, shape, dtype=F32):
    return nc.dram_tensor(name, list(shape), dtype, kind="ExternalOutput").ap()


WP = {
    "inh": (8, 512, 8), "ab": (8, 16, 1), "cq": (8, 384, 1), "kv": (8, 384, 1),
    "inga": (8, 512, 2), "ingb": (8, 512, 2),
    "oa": (8, 512, 2), "ob": (8, 512, 2), "out": (8, 512, 2),
    "uq": (3, 512, 4), "ukT": (1, 2048, 1), "uv": (2, 1024, 1),
    "gu": (8, 512, 11), "down": (NFF, 128, 8),
}
WSRC_SHAPE = {
    "inh": (1024, 4096), "ab": (1024, 16), "cq": (1024, 384), "kv": (1024, 384),
    "inga": (1024, 1024), "ingb": (1024, 1024),
    "oa": (1024, 1024), "ob": (1024, 1024), "out": (1024, 1024),
    "uq": (384, 2048), "ukT": (128, 2048), "uv": (256, 1024),
    "gu": (1024, 5632), "down": (DFF, 1024),
}
WSLOT = 4096

PRM_CONV = 0
PRM_QG = 96
PRM_KVG = 99
PRM_LN1G, PRM_LN1B, PRM_LN2G, PRM_LN2B = 101, 109, 117, 125
PRM_GDNG = 133
PRM_ALOG = 261
PRM_DTB = 262
NPRM = 263

CF_ID, CF_ONE, CF_NEG1 = 0, 128, 256
CF_M64 = 384
CF_M32 = 384 + 6 * 128
CF_IND = CF_M32 + 4 * 128
NCF = CF_IND + 2


def _masks(C):
    idx = np.arange(128)
    same = (idx[:, None] // C) == (idx[None, :] // C)
    U = (same & (idx[:, None] <= idx[None, :])).astype(np.float32)
    Cm = same.astype(np.float32)
    NEGT = np.where(same & (idx[None, :] >= idx[:, None]), 0.0, -1.0e4).astype(np.float32)
    MS = (same & (idx[None, :] > idx[:, None])).astype(np.float32)
    return U, Cm - U, NEGT, MS


def host_consts():
    cf = np.zeros((128, NCF), np.float32)
    cf[:, CF_ID:CF_ID + 128] = np.eye(128, dtype=np.float32)
    cf[:, CF_ONE:CF_ONE + 128] = 1.0
    cf[:, CF_NEG1:CF_NEG1 + 128] = -1.0
    U, CmU, NEGT, MS = _masks(64)
    idx = np.arange(128)
    Cc0 = np.repeat((idx < 64).astype(np.float32)[:, None], 128, 1)
    Cc1 = np.repeat((idx >= 64).astype(np.float32)[:, None], 128, 1)
    for k, m in enumerate((U, CmU, NEGT, MS, Cc0, Cc1)):
        cf[:, CF_M64 + k * 128:CF_M64 + (k + 1) * 128] = m
    for k, m in enumerate(_masks(32)):
        cf[:, CF_M32 + k * 128:CF_M32 + (k + 1) * 128] = m
    cf[:, CF_IND] = (idx < 64)
    cf[:, CF_IND + 1] = (idx >= 64)
    oh = np.zeros((8, 8, 128), np.float32)
    for h in range(8):
        oh[h, h, :] = 1.0
    oh = oh.reshape(8, 1024)
    j = np.arange(128)[:, None, None]
    d = np.arange(4)[None, :, None]
    q = np.arange(512)[None, None, :]
    nega = np.where((d * 128 + j) // 64 <= q // 64, 0.0, NEGBIG).astype(np.float32).reshape(128, 2048)
    half = ROPE // 2
    inv = (10000.0 ** (-np.arange(half, dtype=np.float32) / half)).astype(np.float32)

    def tabs(pos):
        ang = pos.astype(np.float32)[None, :] * inv[:, None]
        c, s = np.cos(ang).astype(np.float32), np.sin(ang).astype(np.float32)
        return np.stack([np.concatenate([c, c], 0), np.concatenate([-s, s], 0)], 0)

    ropeT = tabs(np.arange(SEQ))
    rs1 = tabs(PAST + np.arange(SL))
    ropeS = np.concatenate([rs1, rs1], axis=2)
    return {"cst_f": cf, "oh": oh, "nega": nega, "ropeT": np.ascontiguousarray(ropeT),
            "ropeS": np.ascontiguousarray(ropeS)}


def host_weights(w_in, conv_w, a_log, dt_bias, gdn_norm_g, w_oa, q_norm_g, w_uq, kv_norm_g, w_ukv,
                 w_ob, w_out, ln1_g, ln1_b, w_gu, w_down, ln2_g, ln2_b):
    o = {}
    qkv0, z0, a0, b0, cq0, ckv0, kr0, ga0, gb0 = 0, 3072, 4096, 4104, 4112, 4496, 4752, 4816, 5840
    cols = []
    for h in range(H):
        cols += list(range(qkv0 + h * 128, qkv0 + (h + 1) * 128))
        cols += list(range(qkv0 + 1024 + h * 128, qkv0 + 1024 + (h + 1) * 128))
        cols += list(range(qkv0 + 2048 + h * 128, qkv0 + 2048 + (h + 1) * 128))
        cols += list(range(z0 + h * 128, z0 + (h + 1) * 128))
    o["w_inh"] = np.ascontiguousarray(w_in[:, :, cols])
    o["w_ab"] = np.ascontiguousarray(w_in[:, :, a0:a0 + 16])
    o["w_cq"] = np.ascontiguousarray(w_in[:, :, cq0:cq0 + 384])
    krsw = list(range(kr0 + 32, kr0 + 64)) + list(range(kr0, kr0 + 32))
    o["w_kv"] = np.ascontiguousarray(w_in[:, :, list(range(ckv0, ckv0 + 256)) + list(range(kr0, kr0 + 64)) + krsw])
    o["w_inga"] = np.ascontiguousarray(w_in[:, :, ga0:ga0 + 1024])
    o["w_ingb"] = np.ascontiguousarray(w_in[:, :, gb0:gb0 + 1024])
    o["w_oa"], o["w_ob"], o["w_out"] = w_oa, w_ob, w_out
    cols = []
    for h in range(H):
        b = h * 192
        cols += list(range(b, b + 128)) + list(range(b + 128, b + 192))
        cols += list(range(b + 160, b + 192)) + list(range(b + 128, b + 160))
    o["w_uq"] = np.ascontiguousarray(w_uq[:, :, cols])
    wk = w_ukv.reshape(DEPTH, KVR, H, 256)
    o["w_ukT"] = np.ascontiguousarray(np.transpose(wk[:, :, :, :128], (0, 3, 2, 1)).reshape(DEPTH, 128, H * 256))
    o["w_uv"] = np.ascontiguousarray(wk[:, :, :, 128:].reshape(DEPTH, KVR, H * 128))
    cols = []
    for j in range(NFF):
        cols += list(range(j * 128, (j + 1) * 128)) + list(range(DFF + j * 128, DFF + (j + 1) * 128))
    o["w_gu"] = np.ascontiguousarray(w_gu[:, :, cols])
    o["w_down"] = w_down
    prm = np.zeros((DEPTH, 128, NPRM), np.float32)
    for l in range(DEPTH):
        cw = conv_w[l].reshape(4, 24, 128)
        prm[l, :, PRM_CONV:PRM_CONV + 96] = np.transpose(cw, (2, 1, 0)).reshape(128, 96)
        prm[l, :, PRM_QG:PRM_QG + 3] = q_norm_g[l].reshape(3, 128).T
        prm[l, :, PRM_KVG:PRM_KVG + 2] = kv_norm_g[l].reshape(2, 128).T
        prm[l, :, PRM_LN1G:PRM_LN1G + 8] = ln1_g[l].reshape(8, 128).T
        prm[l, :, PRM_LN1B:PRM_LN1B + 8] = ln1_b[l].reshape(8, 128).T
        prm[l, :, PRM_LN2G:PRM_LN2G + 8] = ln2_g[l].reshape(8, 128).T
        prm[l, :, PRM_LN2B:PRM_LN2B + 8] = ln2_b[l].reshape(8, 128).T
        prm[l, :, PRM_GDNG:PRM_GDNG + 128] = gdn_norm_g[l][None, :]
        prm[l, :8, PRM_ALOG] = a_log[l]
        prm[l, :8, PRM_DTB] = dt_bias[l]
    o["prm"] = prm
    return o


class PBank:
    def __init__(self, ap2d, name, dtype_is_bf16=False):
        self.ap = ap2d
        self.bufs = [Buf(name, excl=True)]
        self.bf = dtype_is_bf16

    def full(self, n=128, T=512, c0=0):
        return V(self.ap[:n, c0:c0 + T], self.bufs)

    def q(self, k, n=128, w=128, r0=0):
        return V(self.ap[r0:r0 + n, k * 128:k * 128 + w], self.bufs)


class Rot:
    def __init__(self, items):
        self.items = items
        self.i = 0

    def next(self):
        x = self.items[self.i % len(self.items)]
        self.i += 1
        return x


def build_program(dbg=None):
    nc = bass.Bass("TRN2", target_bir_lowering=False)
    kb = KB(nc)
    P = kb.P
    xT = dram_in(nc, "xT", [D, SEQ])
    xsT = dram_in(nc, "xsT", [D, 2 * SL])
    sconvT = dram_in(nc, "sconvT", [DEPTH, 2, 128, 72])
    sgdn = dram_in(nc, "sgdn", [DEPTH, 2, 128, 1024])
    ckvT = dram_in(nc, "ckvT", [DEPTH, 2, KVR, PAST])
    ckvN = dram_in(nc, "ckvN", [DEPTH, 2, PAST, KVR])
    krT = dram_in(nc, "krT", [DEPTH, 2, ROPE, PAST])
    wsrc = {n: dram_in(nc, "w_" + n, [DEPTH] + list(WSRC_SHAPE[n])) for n in WP}
    prm_d = dram_in(nc, "prm", [DEPTH, 128, NPRM])
    cst_d = dram_in(nc, "cst_f", [128, NCF])
    oh_d = dram_in(nc, "oh", [8, 1024])
    nega_d = dram_in(nc, "nega", [128, 2048])
    ropeT_d = dram_in(nc, "ropeT", [2, ROPE, SEQ])
    ropeS_d = dram_in(nc, "ropeS", [2, ROPE, 2 * SL])

    yT = dram_out(nc, "yT", [D, SEQ])
    ysT = dram_out(nc, "ysT", [D, 2 * SL])
    o_pconv = dram_out(nc, "p_conv", [DEPTH, 128, 72])
    o_pgdn = dram_out(nc, "p_gdn", [DEPTH, 128, 1024])
    o_pckv = dram_out(nc, "p_ckvT", [DEPTH, KVR, SEQ])
    o_pkr = dram_out(nc, "p_krT", [DEPTH, ROPE, SEQ])
    o_sconv = dram_out(nc, "s_conv", [DEPTH, 2, 128, 72])
    o_sgdn = dram_out(nc, "s_gdn", [DEPTH, 2, 128, 1024])
    o_sckv = dram_out(nc, "s_ckvT", [DEPTH, KVR, 2 * SL])
    o_skr = dram_out(nc, "s_krT", [DEPTH, ROPE, 2 * SL])
    x1T = nc.dram_tensor("x1T", [D, SEQ], F32, kind="Internal").ap()
    xs1T = nc.dram_tensor("xs1T", [D, 2 * SL], F32, kind="Internal").ap()
    wbf = {n: nc.dram_tensor("wb_" + n, [DEPTH, WP[n][2], 128, WP[n][0] * WP[n][1]], BF16, kind="Internal").ap()
           for n in WP}
    dram_buf = {k: Buf(k) for k in ("x1T", "xs1T", "out")}
    wbf_buf = {(n, l, p): Buf("wb") for n in WP for l in range(DEPTH) for p in range(WP[n][2])}

    def DV(ap, key="out"):
        return V(ap, [])

    DO = DV
    xblk_buf = {}

    def DI(ap):
        return V(ap, [])

    banks = [PBank(nc.alloc_psum_tensor("pb%d" % i, [128, 512], F32)[:, :], "pb%d" % i) for i in range(7)]
    bbank = PBank(nc.alloc_psum_tensor("pbb", [128, 1024], BF16)[:, :], "pbb", True)
    btr = Rot([(bbank, k) for k in range(8)])

    cf = kb.alloc([128, NCF], F32, name="cf")
    idb = kb.alloc([128, 128], BF16, name="idb")
    oneb = kb.alloc([128, 128], BF16, name="oneb")
    oh = kb.alloc([128, 1024], F32, name="oh")
    prm = kb.alloc([128, DEPTH, NPRM], F32, name="prm")
    nea = kb.alloc([128, DEPTH], F32, name="nea")
    dummy = kb.alloc([128, 8], F32, name="dummy")
    cst = kb.alloc([128, 24, 3], F32, nb=24, name="cst")
    Sf = kb.alloc([128, 8, 128], F32, nb=8, name="Sf")
    Sb = kb.alloc([128, 8, 128], BF16, nb=8, name="Sb")
    KcT = kb.alloc([128, 2, SEQ], BF16, name="KcT")
    KrT = kb.alloc([128, SEQ], BF16, name="KrT")
    Vtok = kb.alloc([128, SEQ // 128, 256], BF16, name="Vtok")
    wslots = [kb.alloc([128, WSLOT], BF16, name="ws%d" % i) for i in range(3)]
    xbf = [kb.alloc([128, 8, TB], BF16, nb=8, name="xbf%d" % i) for i in range(1)]
    mT = kb.alloc([128, 8, TB], BF16, nb=8, name="mT")
    scratch_base = kb.off
    kb.prev_phase = []
    kb.cur_phase = []

    def phase_begin(keep=()):
        kb.prev_phase = [t for t in kb.cur_phase if t not in keep]
        kb.cur_phase = list(keep)
        kb.off = scratch_base
        for t in keep:
            kb.off = max(kb.off, t.end_off)
        kb.new_phase = []

    def sal(shape, dtype, nb=1, name="s"):
        t = kb.alloc(shape, dtype, nb, name)
        t.end_off = kb.off
        kb.cur_phase.append(t)
        kb.new_phase.append(t)
        return t

    def phase_ready():
        kb.barrier(kb.prev_phase, kb.new_phase, dummy.all())

    def cfv(col, n=128, w=128):
        return V(cf.ap[:n, col:col + w], cf.bufs)

    ident_f = lambda n=128: cfv(CF_ID, n, n)
    ident_d = (lambda n=128: V(idb.ap[:n, :n], idb.bufs)) if DBL_DT == BF16 else ident_f
    ones_f = lambda n=128, w=128: cfv(CF_ONE, n, w)
    neg1_f = lambda n=128, w=128: cfv(CF_NEG1, n, w)

    def prmv(l, col, n=128, w=1):
        return V(prm.ap[:n, l, col:col + w], prm.bufs)

    def act_sigmoid(out, x, tmp):
        kb.act(tmp, x, AF.Exp, scale=-1.0)
        kb.act(tmp, tmp, AF.Ln, bias=1.0)
        kb.act(out, tmp, AF.Exp, scale=-1.0)

    def act_rsqrt(out, x, scale, tmp=None):
        t = tmp if tmp is not None else out
        kb.act(t, x, AF.Ln, bias=EPS, scale=scale)
        kb.act(out, t, AF.Exp, scale=-0.5)

    kb.dma(cf.all(), DI(cst_d))
    kb.dma(oh[0:8, :], DI(oh_d))
    kb.dma(prm.all(), DI(prm_d.rearrange("l p n -> p l n")))
    kb.dma(idb.all(), DI(cst_d[:, CF_ID:CF_ID + 128]), eng="gpsimd")
    kb.dma(oneb.all(), DI(cst_d[:, CF_ONE:CF_ONE + 128]), eng="gpsimd")
    for l in range(DEPTH):
        kb.act(nea[0:8, l:l + 1], prmv(l, PRM_ALOG, 8), AF.Exp)
        kb.ts(nea[0:8, l:l + 1], nea[0:8, l:l + 1], -1.0, ALU.mult)
    for l in range(DEPTH):
        for n, (nk, w, npc) in WP.items():
            for p in range(npc):
                src = wsrc[n][l, :, p * w:(p + 1) * w]
                if nk > 1:
                    src = src.rearrange("(k q) w -> q k w", q=128)
                    dst = wbf[n][l, p].rearrange("q (k w) -> q k w", k=nk)
                else:
                    dst = wbf[n][l, p]
                kb.dma(V(dst, wbf_buf[(n, l, p)]), DI(src), eng="gpsimd")

    class WStream:
        AHEAD = 1

        def __init__(self):
            self.seq = []
            self.issued = 0
            self.cons = 0

        def _issue(self, i):
            l, n, p = self.seq[i]
            nk, w, npc = WP[n]
            slot = wslots[i % 3]
            kb.dma(V(slot.ap[:, 0:nk * w], slot.bufs), V(wbf[n][l, p], wbf_buf[(n, l, p)]))

        def get(self, l, n, p=0):
            i = self.cons
            assert self.seq[i] == (l, n, p), (self.seq[i], (l, n, p))
            self.cons += 1
            while self.issued < min(len(self.seq), i + 1 + self.AHEAD):
                self._issue(self.issued)
                self.issued += 1
            nk, w, npc = WP[n]
            slot = wslots[i % 3]

            def view(kc, c0, c1, rows=128):
                return V(slot.ap[:rows, kc * w + c0:kc * w + c1], slot.bufs)
            return view

    wstream = WStream()
    kb.wstream = wstream

    def block_seq(l, phases):
        q = [(l, "ab", 0)] + [(l, "inh", h) for h in range(H)]
        q += [(l, "inga", 0), (l, "oa", 0), (l, "inga", 1), (l, "oa", 1)]
        if phases <= 1:
            return q
        q += [(l, "cq", 0), (l, "kv", 0)] + [(l, "uq", i) for i in range(4)]
        q += [(l, "ingb", 0), (l, "ob", 0), (l, "ingb", 1), (l, "ob", 1)]
        if phases <= 2:
            return q
        q += [(l, "out", 0), (l, "out", 1)]
        if phases <= 3:
            return q
        q += [(l, "gu", i) for i in range(11)] + [(l, "down", i) for i in range(8)]
        return q

    kb.block_seq = block_seq

    class WS:
        def __init__(self, l):
            self.l = l

        def get(self, n, p=0):
            return wstream.get(self.l, n, p)

    def proj(out_v, wv, c0, c1, rhs_fn, nk, rows=128):
        for kc in range(nk):
            kb.mm(out_v, wv(kc, c0, c1, rows), rhs_fn(kc), start=(kc == 0), stop=(kc == nk - 1))

    def block(l, kind, blk, phases=9):
        if kind == "p":
            T, Ls, nseg, n, Cc, L = TB, TB, 1, 128, 64, 5
            tiles = [(i * 128, 128, 0) for i in range(4)]
            xsrc = (xT if l == 0 else x1T)[:, blk * TB:(blk + 1) * TB]
            xdst = (x1T if l == 0 else yT)[:, blk * TB:(blk + 1) * TB]
            xkey_in, xkey_out = ("x1T" if l == 1 else None), ("x1T" if l == 0 else "out")
            MB = CF_M64
            pos0 = blk * TB
        else:
            T, Ls, nseg, n, Cc, L = 2 * SL, SL, 2, 32, 32, 4
            tiles = [(0, 32, 0), (32, 32, 1)]
            xsrc = xsT if l == 0 else xs1T
            xdst = xs1T if l == 0 else ysT
            xkey_in, xkey_out = ("xs1T" if l == 1 else None), ("xs1T" if l == 0 else "out")
            MB = CF_M32
            pos0 = 0
        nch = n // Cc
        ntile = len(tiles)
        xbk = [xblk_buf.setdefault((kind, blk, c), Buf("xblk")) for c in range(8)]
        xin = lambda ap, c=None: V(ap, (xbk if c is None else [xbk[c]])) if l == 1 else DI(ap)
        xout = lambda ap, c: V(ap, [xbk[c]]) if l == 0 else DI(ap)
        U_m, CmU_m, NEGT_m, MS_m = (cfv(MB + k * 128, n, n) for k in range(4))
        xb = xbf[0]
        kb.rr += 1
        ws = WS(l)
        fullrot = Rot([banks[0], banks[1]])
        qrot = Rot([(banks[b], 0) for b in (2, 3, 4, 5, 6)])
        pfull = lambda rows=128, TT=None: fullrot.next().full(rows, TT or T)

        def pq(rows=128, w=128, r0=0):
            b, k = qrot.next()
            return b.q(k, rows, w, r0)

        def pbt(rows=128, w=128):
            b, k = btr.next()
            return b.q(k, rows, w)

        xrhs = lambda kc: xb[:, kc, 0:T]
        kb.dma(V(xb.ap[:, :, 0:T], xb.bufs), xin(xsrc.rearrange("(k q) t -> q k t", q=128)), eng="gpsimd")

        phase_begin()
        gT = sal([128, T], F32, name="gT")
        bT = sal([128, T], F32, name="bT")
        gtok = sal([128, ntile, 8], F32, name="gtok")
        btok = sal([128, ntile, 8], F32, name="btok")
        eG = sal([128, ntile, 8], F32, name="eG")
        eGL = sal([128, ntile, 8], F32, name="eGL")
        bexpG = sal([128, ntile, 8], F32, name="bexpG")
        eGLm = sal([128, ntile, 16], F32, name="eGLm")
        glbc = sal([128, ntile * 2, 8], F32, name="glbc")
        oaT = sal([128, 8, T], BF16, nb=8, name="oaT")
        raw_ = [sal([128, nseg * (3 + Ls)], F32, name="raw") for _ in range(2)]
        acc_ = [sal([128, T], F32, name="acc") for _ in range(2)]
        Vb16 = sal([128, T], BF16, name="Vb16")
        sq_ = [sal([128, T], BF16, name="sq") for _ in range(2)]
        rs_ = [sal([128, T], F32, name="rs") for _ in range(2)]
        Qn2 = [sal([128, T], BF16, name="Qn") for _ in range(2)]
        sz2 = [sal([128, T], BF16, name="sz") for _ in range(2)]
        Kn = sal([128, T], BF16, name="Kn")
        Kb = sal([128, T], BF16, name="Kb")
        NTL = ntile
        gU4 = sal([128, NTL, 128], F32, name="gU4")
        Dec4 = sal([128, NTL, 128], F32, name="Dec4")
        NRM = [sal([128, NTL, 384], DBL_DT, name="NRM%d" % i) for i in range(2)]
        Kbg4 = sal([128, NTL, 128], BF16, name="Kbg4")
        tc = [{nm: sal([128, NTL, 128], BF16, name=nm) for nm in ("QKt", "Tt", "kd0", "kd1", "Vbt", "mwT")}
              for hp in range(2)]
        cs_ = []
        for s in range(2):
            d = {"vn": sal([128, 128], BF16, name="vn"), "on": sal([128, 128], BF16, name="on"),
                 "otok": sal([128, 128], F32, name="otok"), "t1": sal([128, 128], F32, name="t1"),
                 "ssq": sal([128, 2], F32, name="ssq")}
            cs_.append(d)
        sg = sal([128, T], F32, name="sg")
        phase_ready()
        for d in cs_:
            kb.memset(d["vn"].all(), 0.0)

        wab = ws.get("ab")
        pa = pfull(8)
        proj(pa, wab, 0, 8, xrhs, 8)
        kb.act(gT[0:8, :], pa, AF.Exp, bias=prmv(l, PRM_DTB, 8))
        kb.act(gT[0:8, :], gT[0:8, :], AF.Ln, bias=1.0)
        kb.ts(gT[0:8, :], gT[0:8, :], V(nea.ap[0:8, l:l + 1], nea.bufs), ALU.mult)
        pb_ = pfull(8)
        proj(pb_, wab, 8, 16, xrhs, 8)
        act_sigmoid(bT[0:8, :], pb_, bT[0:8, :])
        for ti, (t0, nn, sidx) in enumerate(tiles):
            p1 = pq(n, 8)
            kb.tr(p1, gT[0:8, t0:t0 + n], ident_f(8))
            kb.copy(gtok[:n, ti, :], p1, eng="scalar")
            p2 = pq(n, 8)
            kb.tr(p2, bT[0:8, t0:t0 + n], ident_f(8))
            kb.copy(btok[:n, ti, :], p2, eng="vector")
            p3 = pq(n, 8)
            kb.mm(p3, U_m, gtok[:n, ti, :])
            kb.act(eG[:n, ti, :], p3, AF.Exp)
            p4 = pq(n, 8)
            kb.mm(p4, CmU_m, gtok[:n, ti, :])
            kb.act(eGL[:n, ti, :], p4, AF.Exp)
            kb.tt(bexpG[:n, ti, :], btok[:n, ti, :], eG[:n, ti, :], ALU.mult)
            for c in range(nch):
                if nch == 1:
                    kb.copy(eGLm[:n, ti, c * 8:(c + 1) * 8], eGL[:n, ti, :])
                    lhs = ones_f(n, 128)
                else:
                    kb.ts(eGLm[:n, ti, c * 8:(c + 1) * 8], eGL[:n, ti, :], cfv(CF_IND + c, n, 1), ALU.mult)
                    lhs = cfv(CF_M64 + (4 + c) * 128, n, 128)
                p5 = pq(128, 8)
                kb.mm(p5, lhs, gtok[:n, ti, :])
                kb.act(glbc[:, ti * 2 + c, :], p5, AF.Exp)

        if kind == "p" and blk == 0:
            kb.memset(cst.all(), 0.0)
            kb.memset(Sf.all(), 0.0)
            kb.memset(Sb.all(), 0.0)

        def sqv(d, nm):
            return V(d[nm].ap[:n, :n], d[nm].bufs)

        def nr(d, which, half):
            t = d["NR%d" % which]
            return V(t.ap[:n, half * n:(half + 1) * n], t.bufs)

        def pq2(rows, w):
            b_, k_ = qrot.next()
            return V(b_.ap[:rows, 0:w], b_.bufs)

        def rowv(d, nm, rows=None):
            return V(d[nm].ap[:(rows or n), :], d[nm].bufs)

        def gen_AB(h):
            hp = h % 2
            Qn, sz = Qn2[hp], sz2[hp]
            wh = ws.get("inh", h)
            for j in range(3):
                raw, acc, sq, rs = raw_[j % 2], acc_[j % 2], sq_[j % 2], rs_[j % 2]
                r3 = V(raw.ap.rearrange("p (s c) -> p s c", s=nseg), raw.bufs)
                a3 = V(acc.ap.rearrange("p (s c) -> p s c", s=nseg), acc.bufs)
                ch = j * 8 + h
                pj = pfull()
                proj(pj, wh, j * 128, (j + 1) * 128, xrhs, 8)
                yield
                kb.copy(V(r3.ap[:, :, 3:3 + Ls], r3.bufs), V(pj.ap.rearrange("p (s c) -> p s c", s=nseg), pj.bufs),
                        eng="scalar")
                if kind == "p":
                    kb.copy(V(r3.ap[:, 0, 0:3], r3.bufs), cst[:, ch, :], eng="gpsimd")
                    kb.copy(cst[:, ch, :], V(r3.ap[:, 0, Ls:Ls + 3], r3.bufs), eng="gpsimd")
                    if blk == NBLK - 1:
                        kb.dma(DV(o_pconv[l, :, ch * 3:ch * 3 + 3]), cst[:, ch, :])
                else:
                    for s in range(nseg):
                        kb.dma(V(r3.ap[:, s, 0:3], r3.bufs), DI(sconvT[l, s, :, ch * 3:ch * 3 + 3]))
                        kb.dma(DV(o_sconv[l, s, :, ch * 3:ch * 3 + 3]), V(r3.ap[:, s, Ls:Ls + 3], r3.bufs))
                yield
                kb.ts(a3, V(r3.ap[:, :, 0:Ls], r3.bufs), prmv(l, PRM_CONV + ch * 4 + 0), ALU.mult)
                for jj in range(1, 4):
                    kb.stt(a3, V(r3.ap[:, :, jj:jj + Ls], r3.bufs), prmv(l, PRM_CONV + ch * 4 + jj), a3,
                           ALU.mult, ALU.add)
                    if jj % 2 == 1:
                        yield
                act_sigmoid(rs.all(), acc.all(), rs.all())
                yield
                if j == 2:
                    kb.tt(Vb16.all(), acc.all(), rs.all(), ALU.mult)
                else:
                    kb.tt(acc.all(), acc.all(), rs.all(), ALU.mult)
                    kb.act(sq.all(), acc.all(), AF.Square)
                    pss = pfull()
                    kb.mm(pss, oneb.all(), sq.all())
                    yield
                    act_rsqrt(rs.all(), pss, 1.0)
                    if j == 0:
                        kb.stt(Qn.all(), acc.all(), DK ** -0.5, rs.all(), ALU.mult, ALU.mult)
                    else:
                        kb.tt(Kn.all(), acc.all(), rs.all(), ALU.mult)
                yield
            pz = pfull()
            proj(pz, wh, 384, 512, xrhs, 8)
            yield
            rs = rs_[1]
            act_sigmoid(rs.all(), pz, rs.all())
            kb.tt(sz.all(), pz, rs.all(), ALU.mult)
            yield
            pbb = pfull()
            kb.mm(pbb, V(oh.ap[0:8, h * 128:(h + 1) * 128], oh.bufs), bT[0:8, :])
            kb.tt(Kb.all(), Kn.all(), pbb, ALU.mult)
            yield

            oN, oR, oM = 0, n, 2 * n
            tcd = tc[hp]

            def bc_mat(v):
                return V(v.ap.unsqueeze(1).to_broadcast([n, NTL, n]), v.bufs)

            def bc_col(t, col, w):
                return V(t.ap[:n, :, col:col + 1].to_broadcast([n, NTL, w]), t.bufs)

            def t4(t, c0, c1, rows=None):
                return V(t.ap[:(rows or n), :, c0:c1], t.bufs)

            def pbank(w, rows=None):
                b_, k_ = qrot.next()
                r_ = rows or n
                return V(b_.ap[:r_, 0:NTL * w].rearrange("p (t c) -> p t c", t=NTL), b_.bufs)

            def pslice(pb, ti, c0=None, c1=None):
                return V(pb.ap[:, ti, :] if c0 is None else pb.ap[:, ti, c0:c1], pb.bufs)

            def tslice(t, ti, c0, c1, rows=None):
                return V(t.ap[:(rows or n), ti, c0:c1], t.bufs)

            tsl = [slice(t0, t0 + n) for (t0, _, _) in tiles]
            kb.tt(t4(gU4, 0, n), bc_mat(U_m), bc_col(gtok, h, n), ALU.mult)
            pb = pbank(n)
            for ti in range(NTL):
                o_ = pslice(pb, ti)
                g_ = tslice(gU4, ti, 0, n)
                kb.mm(o_, ones_f(n, n), g_, start=True, stop=False)
                kb.mm(o_, g_, neg1_f(n, n), start=False, stop=False)
                kb.mm(o_, ident_f(n), NEGT_m, start=False, stop=True)
            kb.act(t4(Dec4, 0, n), pb, AF.Exp)
            yield
            pb = pbank(n)
            for ti in range(NTL):
                kb.mm(pslice(pb, ti), Kn[:, tsl[ti]], Qn[:, tsl[ti]])
            kb.tt(t4(tcd["QKt"], 0, n), pb, t4(Dec4, 0, n), ALU.mult)
            kb.tt(t4(Dec4, 0, n), t4(Dec4, 0, n), bc_mat(MS_m), ALU.mult)
            pb = pbank(n)
            for ti in range(NTL):
                kb.mm(pslice(pb, ti), Kn[:, tsl[ti]], Kb[:, tsl[ti]])
            kb.stt(t4(NRM[0], oN, oN + n), pb, -1.0, t4(Dec4, 0, n), ALU.mult, ALU.mult)
            yield
            pb = pbank(n)
            for ti in range(NTL):
                kb.mm(pslice(pb, ti), tslice(NRM[0], ti, oN, oN + n), ident_d(n))
            kb.copy(t4(NRM[0], oM, oM + n), pb, eng="scalar")
            yield
            pb = pbank(n)
            for ti in range(NTL):
                kb.mm(pslice(pb, ti), tslice(NRM[0], ti, oM, oM + n), tslice(NRM[0], ti, oN, oN + n))
            kb.copy(t4(NRM[1], oN, oN + n), pb, eng="scalar")
            kb.tt(t4(NRM[1], oR, oR + n), t4(NRM[0], oN, oN + n), bc_mat(ident_f(n)), ALU.add)
            pb = pbank(n)
            for ti in range(NTL):
                kb.mm(pslice(pb, ti), tslice(NRM[0], ti, oN, oN + n), tslice(NRM[0], ti, oM, oM + n))
            kb.copy(t4(NRM[1], oM, oM + n), pb, eng="scalar")
            yield
            for k in range(1, L):
                cur, nxt = NRM[k % 2], NRM[(k + 1) % 2]
                if k < L - 1:
                    per = max(1, 512 // (2 * n))
                    for g0 in range(0, NTL, per):
                        g1 = min(NTL, g0 + per)
                        b_, k_ = qrot.next()
                        pg = V(b_.ap[:n, 0:(g1 - g0) * 2 * n].rearrange("p (t c) -> p t c", t=g1 - g0), b_.bufs)
                        for ti in range(g0, g1):
                            kb.mm(V(pg.ap[:, ti - g0, :], pg.bufs), tslice(cur, ti, oM, oM + n),
                                  tslice(cur, ti, oN, oN + 2 * n))
                        kb.copy(V(nxt.ap[:n, g0:g1, oN:oN + n], nxt.bufs), V(pg.ap[:, :, 0:n], pg.bufs), eng="scalar")
                        kb.tt(V(nxt.ap[:n, g0:g1, oR:oR + n], nxt.bufs), V(cur.ap[:n, g0:g1, oR:oR + n], cur.bufs),
                              V(pg.ap[:, :, n:2 * n], pg.bufs), ALU.add)
                else:
                    pb = pbank(n)
                    for ti in range(NTL):
                        kb.mm(pslice(pb, ti), tslice(cur, ti, oM, oM + n), tslice(cur, ti, oR, oR + n))
                    kb.tt(t4(nxt, oR, oR + n), t4(cur, oR, oR + n), pb, ALU.add)
                pb = pbank(n)
                for ti in range(NTL):
                    kb.mm(pslice(pb, ti), tslice(cur, ti, oN, oN + n), tslice(cur, ti, oM, oM + n))
                kb.copy(t4(nxt, oM, oM + n), pb, eng="scalar")
                yield
            cur = NRM[L % 2]
            pb = pbank(n)
            for ti in range(NTL):
                kb.mm(pslice(pb, ti), tslice(cur, ti, oM, oM + n), tslice(cur, ti, oR, oR + n))
            kb.tt(t4(tcd["Tt"], 0, n), t4(cur, oR, oR + n), pb, ALU.add)
            yield
            b_, k_ = btr.next()
            pkt = V(b_.ap[:n, 0:NTL * 128].rearrange("p (t c) -> p t c", t=NTL), b_.bufs)
            for ti in range(NTL):
                kb.tr(V(pkt.ap[:, ti, :], pkt.bufs), Kn[:, tsl[ti]], idb.all())
            kb.tt(t4(Kbg4, 0, 128), pkt, bc_col(bexpG, h, 128), ALU.mult)
            for c in range(nch):
                kb.tt(t4(tcd["kd%d" % c], 0, 128), pkt, bc_col(eGLm, c * 8 + h, 128), ALU.mult)
            b_, k_ = btr.next()
            pvt = V(b_.ap[:n, 0:NTL * 128].rearrange("p (t c) -> p t c", t=NTL), b_.bufs)
            for ti in range(NTL):
                kb.tr(V(pvt.ap[:, ti, :], pvt.bufs), Vb16[:, tsl[ti]], idb.all())
            kb.tt(t4(tcd["Vbt"], 0, 128), pvt, bc_col(btok, h, 128), ALU.mult)
            yield
            pb = pbank(n, rows=128)
            for ti in range(NTL):
                kb.mm(pslice(pb, ti), tslice(Kbg4, ti, 0, 128), tslice(tcd["Tt"], ti, 0, n))
            kb.act(t4(tcd["mwT"], 0, n, rows=128), pb, AF.Copy, scale=-1.0)
            yield

        def gen_C(h):
            hp = h % 2
            Qn, sz = Qn2[hp], sz2[hp]
            for ti, (t0, nn, sidx) in enumerate(tiles):
                e, d = tc[hp], cs_[ti % 2]
                sl = slice(t0, t0 + n)
                if kind == "s":
                    kb.dma(Sf[:, h, :], DI(sgdn[l, sidx, :, h * 128:(h + 1) * 128]))
                    kb.copy(Sb[:, h, :], Sf[:, h, :], eng="gpsimd")
                for c in range(nch):
                    r0 = c * Cc
                    cs = slice(r0, r0 + Cc)
                    pv = pq(Cc, 128, r0)
                    kb.mm(pv, V(e["Tt"].ap[:n, ti, cs], e["Tt"].bufs), V(e["Vbt"].ap[:n, ti, :], e["Vbt"].bufs), start=True, stop=False)
                    kb.mm(pv, V(e["mwT"].ap[:, ti, cs], e["mwT"].bufs), Sb[:, h, :], start=False, stop=True)
                    po1 = pq(Cc, 128, r0)
                    kb.mm(po1, Qn[:, t0 + r0:t0 + r0 + Cc], Sb[:, h, :])
                    kb.copy(V(d["vn"].ap[cs, :], d["vn"].bufs), pv, eng="scalar")
                    kb.act(V(d["t1"].ap[cs, :], d["t1"].bufs), po1, AF.Copy, scale=eG[cs, ti, h:h + 1])
                    yield
                    po2 = pq(Cc, 128, r0)
                    kb.mm(po2, V(e["QKt"].ap[:n, ti, cs], e["QKt"].bufs), rowv(d, "vn"))
                    pS = pq(128, 128)
                    kb.mm(pS, V(e["kd%d" % c].ap[:n, ti, :], e["kd%d" % c].bufs), rowv(d, "vn"))
                    kb.tt(V(d["otok"].ap[cs, :], d["otok"].bufs), V(d["t1"].ap[cs, :], d["t1"].bufs), po2, ALU.add)
                    kb.stt(Sb[:, h, :], Sf[:, h, :], glbc[:, ti * 2 + c, h:h + 1], pS, ALU.mult, ALU.add)
                    kb.stt(Sf[:, h, :], Sf[:, h, :], glbc[:, ti * 2 + c, h:h + 1], pS, ALU.mult, ALU.add)
                    yield
                if kind == "s":
                    kb.dma(DV(o_sgdn[l, sidx, :, h * 128:(h + 1) * 128]), Sf[:, h, :])
                elif blk == NBLK - 1 and ti == ntile - 1:
                    kb.dma(DV(o_pgdn[l, :, h * 128:(h + 1) * 128]), Sf[:, h, :])
                ot, t1 = rowv(d, "otok"), rowv(d, "t1")
                ssq = V(d["ssq"].ap[:n, 0:1], d["ssq"].bufs)
                kb.act(t1, ot, AF.Square, accum=ssq)
                act_rsqrt(ssq, ssq, 1.0 / 128)
                kb.stt(rowv(d, "on"), ot, ssq, prmv(l, PRM_GDNG, n, 128), ALU.mult, ALU.mult)
                pot = pbt(128, n)
                kb.tr(pot, rowv(d, "on"), V(idb.ap[:n, :n], idb.bufs))
                kb.tt(oaT[:, h, sl], pot, sz[:, sl], ALU.mult)
                yield

        def run_interleaved(gens, weights=None):
            pairs = [(g, (weights[i] if weights else 1)) for i, g in enumerate(gens) if g is not None]
            while pairs:
                for g, w in list(pairs):
                    for _ in range(w):
                        try:
                            next(g)
                        except StopIteration:
                            pairs = [p for p in pairs if p[0] is not g]
                            break

        run_interleaved([gen_AB(0)])
        for h in range(H):
            run_interleaved([gen_C(h), gen_AB(h + 1) if h + 1 < H else None], weights=[1, 2])
        for c in range(8):
            if c % 4 == 0:
                wga = ws.get("inga", c // 4)
                woa = ws.get("oa", c // 4)
            pg = pfull()
            proj(pg, wga, (c % 4) * 128, (c % 4 + 1) * 128, xrhs, 8)
            act_sigmoid(sg.all(), pg, sg.all())
            py = pfull()
            proj(py, woa, (c % 4) * 128, (c % 4 + 1) * 128, lambda kc: oaT[:, kc, :], 8)
            kb.tt(mT[:, c, 0:T], py, sg.all(), ALU.mult)

        if phases <= 1:
            return

        phase_begin()
        CCt = sal([128, T], F32, name="CC")
        SSt = sal([128, T], F32, name="SS")
        if kind == "p":
            NEGA = sal([128, 4, 512], BF16, name="NEGA")
        cqn = sal([128, 3, T], BF16, nb=3, name="cqn")
        obT = sal([128, 8, T], BF16, nb=8, name="obT")
        sg2 = sal([128, T], F32, name="sg2")
        tmpf = sal([128, T], F32, name="tmpf")
        wukb = sal([128, 2048], BF16, name="wukb")
        wuvb = sal([128, 2, 1024], BF16, name="wuvb")
        if kind == "s":
            KnT = sal([128, 2, T], BF16, name="KnT")
            KrnT = sal([128, T], BF16, name="KrnT")
            Vn = sal([128, 2, 256], BF16, name="Vn")
        keep = list(kb.cur_phase)
        cqr = sal([128, 3, T], F32, nb=3, name="cqr")
        ckr = sal([128, 2, T], F32, nb=2, name="ckr")
        sqm = sal([128, 3, T], BF16, nb=3, name="sqm")
        rs2 = sal([128, T], F32, name="rs2")
        kro = sal([128, T], F32, name="kro")
        ktmp = sal([128, T], F32, name="ktmp")
        phase_ready()
        fullrot = Rot([banks[0], banks[1]])
        pfull = lambda rows=128, TT=None: fullrot.next().full(rows, TT or T)
        if kind == "p":
            kb.dma(CCt[0:64, :], DI(ropeT_d[0, :, pos0:pos0 + T]))
            kb.dma(SSt[0:64, :], DI(ropeT_d[1, :, pos0:pos0 + T]))
            kb.dma(V(NEGA.ap.rearrange("p a b -> p (a b)"), NEGA.bufs), DI(nega_d), eng="gpsimd")
        else:
            kb.dma(CCt[0:64, :], DI(ropeS_d[0]))
            kb.dma(SSt[0:64, :], DI(ropeS_d[1]))
        kb.dma(wukb.all(), V(wbf["ukT"][l, 0], wbf_buf[("ukT", l, 0)]))
        kb.dma(V(wuvb.ap.rearrange("p a b -> p (a b)"), wuvb.bufs), V(wbf["uv"][l, 0], wbf_buf[("uv", l, 0)]))
        wcq = ws.get("cq")
        for j in range(3):
            pc = pfull()
            proj(pc, wcq, j * 128, (j + 1) * 128, xrhs, 8)
            kb.copy(cqr[:, j, :], pc, eng="scalar")
            kb.act(sqm[:, j, :], pc, AF.Square)
        pss = pfull()
        for j in range(3):
            kb.mm(pss, oneb.all(), sqm[:, j, :], start=(j == 0), stop=(j == 2))
        kb.act(rs2.all(), pss, AF.Sqrt, bias=EPS, scale=1.0 / QR)
        kb.recip(rs2.all(), rs2.all())
        for j in range(3):
            kb.stt(cqn[:, j, :], cqr[:, j, :], prmv(l, PRM_QG + j), rs2.all(), ALU.mult, ALU.mult)
        wkv = ws.get("kv")
        for j in range(2):
            pc = pfull()
            proj(pc, wkv, j * 128, (j + 1) * 128, xrhs, 8)
            kb.copy(ckr[:, j, :], pc, eng="scalar")
            kb.act(sqm[:, j, :], pc, AF.Square)
        pss = pfull()
        for j in range(2):
            kb.mm(pss, oneb.all(), sqm[:, j, :], start=(j == 0), stop=(j == 1))
        kb.act(rs2.all(), pss, AF.Sqrt, bias=EPS, scale=1.0 / KVR)
        kb.recip(rs2.all(), rs2.all())
        for j in range(2):
            kb.stt(ckr[:, j, :], ckr[:, j, :], prmv(l, PRM_KVG + j), rs2.all(), ALU.mult, ALU.mult)
            if kind == "p":
                kb.dma(DO(o_pckv[l, j * 128:(j + 1) * 128, pos0:pos0 + T]), ckr[:, j, :])
                kb.copy(V(KcT.ap[:, j, pos0:pos0 + T], KcT.bufs), ckr[:, j, :], eng="gpsimd")
            else:
                kb.dma(DO(o_sckv[l, j * 128:(j + 1) * 128, :]), ckr[:, j, :])
                kb.copy(KnT[:, j, :], ckr[:, j, :], eng="gpsimd")
        pA = pfull(64)
        proj(pA, wkv, 256, 320, xrhs, 8)
        pB = pfull(64)
        proj(pB, wkv, 320, 384, xrhs, 8)
        kb.tt(kro[0:64, :], pA, CCt[0:64, :], ALU.mult)
        kb.tt(ktmp[0:64, :], pB, SSt[0:64, :], ALU.mult)
        kb.tt(kro[0:64, :], kro[0:64, :], ktmp[0:64, :], ALU.add, eng="gpsimd")
        if kind == "p":
            kb.dma(DO(o_pkr[l, :, pos0:pos0 + T]), kro[0:64, :])
            kb.copy(V(KrT.ap[0:64, pos0:pos0 + T], KrT.bufs), kro[0:64, :], eng="gpsimd")
            for t in range(4):
                for j in range(2):
                    pt_ = pbt(128, 128)
                    kb.tr(pt_, V(KcT.ap[:, j, pos0 + t * 128:pos0 + (t + 1) * 128], KcT.bufs), idb.all())
                    kb.copy(V(Vtok.ap[:, blk * 4 + t, j * 128:(j + 1) * 128], Vtok.bufs), pt_,
                            eng=("scalar" if j == 0 else "vector"))
        else:
            kb.dma(DO(o_skr[l]), kro[0:64, :])
            kb.copy(KrnT[0:64, :], kro[0:64, :], eng="gpsimd")
            for s in range(2):
                for j in range(2):
                    pt_ = pbt(32, 128)
                    kb.tr(pt_, KnT[:, j, s * 32:(s + 1) * 32], idb.all())
                    kb.copy(V(Vn.ap[0:32, s, j * 128:(j + 1) * 128], Vn.bufs), pt_,
                            eng=("scalar" if j == 0 else "vector"))

        phase_begin(keep=keep)
        NQ = 2
        hd = []
        for s in range(NQ):
            hd.append({"qn": sal([128, T], BF16, name="qn"), "qr": sal([128, T], F32, name="qr"),
                       "qt": sal([128, T], F32, name="qt"), "qrb": sal([128, T], BF16, name="qrb"),
                       "qlat": sal([128, 2, T], BF16, name="qlat")})
        rden = sal([128, T if kind == "p" else 256], F32, name="rden")
        if kind == "p":
            Pacc = sal([128, T], F32, name="Pacc")
            Pab = sal([128, T], BF16, name="Pab")
        if kind == "p":
            PT = [sal([128, T], BF16, name="PT") for _ in range(4)]
            olat = [sal([128, 2, T], BF16, name="olat") for _ in range(2)]
        else:
            PT = [sal([128, 256], BF16, name="PT") for _ in range(3)]
            qlS = sal([128, 4, 256], BF16, name="qlS")
            qrS = sal([128, 2, 256], BF16, name="qrS")
            olS = sal([128, 2, 256], BF16, name="olS")
            spb = [(sal([128, 2, 1024], BF16, name="Ksp"), sal([128, 1024], BF16, name="Krp"),
                    sal([128, 8, 256], BF16, name="Vsp")) for _ in range(2)]
        phase_ready()
        srot = Rot([banks[2], banks[3], banks[4]] if kind == "p" else [banks[3], banks[4]])
        pO0, pO1, pD = banks[5], banks[6], (banks[0] if kind == "p" else banks[2])
        wuk = lambda c0, c1: V(wukb.ap[:, c0:c1], wukb.bufs)
        wuvv = lambda j, c0, c1: V(wuvb.ap[:, j, c0:c1], wuvb.bufs)
        crhs = lambda kc: cqn[:, kc, :]
        for h in range(H):
            d = hd[h % NQ]
            if h % 2 == 0:
                wuq = ws.get("uq", h // 2)
            b0 = (h % 2) * 256
            pn = pfull()
            proj(pn, wuq, b0, b0 + 128, crhs, 3)
            kb.copy(d["qn"].all(), pn, eng="scalar")
            pA = pfull(64)
            proj(pA, wuq, b0 + 128, b0 + 192, crhs, 3)
            pB = pfull(64)
            proj(pB, wuq, b0 + 192, b0 + 256, crhs, 3)
            kb.tt(d["qr"][0:64, :], pA, CCt[0:64, :], ALU.mult)
            kb.tt(d["qt"][0:64, :], pB, SSt[0:64, :], ALU.mult)
            if kind == "p":
                kb.tt(d["qrb"][0:64, :], d["qr"][0:64, :], d["qt"][0:64, :], ALU.add, eng="gpsimd")
            else:
                for s in range(2):
                    kb.tt(qrS[0:64, s, h * 32:(h + 1) * 32], d["qr"][0:64, s * 32:(s + 1) * 32],
                          d["qt"][0:64, s * 32:(s + 1) * 32], ALU.add, eng="gpsimd")
            for j in range(2):
                pl = pfull()
                kb.mm(pl, wuk(h * 256 + j * 128, h * 256 + (j + 1) * 128), d["qn"].all())
                if kind == "p":
                    kb.copy(d["qlat"][:, j, :], pl, eng=("scalar" if j == 0 else "vector"))
                else:
                    for s in range(2):
                        kb.copy(qlS[:, s * 2 + j, h * 32:(h + 1) * 32],
                                V(pl.ap[:, s * 32:(s + 1) * 32], pl.bufs), eng=("scalar" if s == 0 else "vector"))
            if kind != "p":
                continue
            nkt = blk * 4 + 4
            ol = olat[h % 2]

            def qk(kt):
                dgn = kt - blk * 4
                q0 = dgn * 128 if dgn > 0 else 0
                pS = srot.next().full(128, T - q0, q0)
                ks = slice(kt * 128, (kt + 1) * 128)
                kb.mm(pS, V(KcT.ap[:, 0, ks], KcT.bufs), d["qlat"][:, 0, q0:T], start=True, stop=False)
                kb.mm(pS, V(KcT.ap[:, 1, ks], KcT.bufs), d["qlat"][:, 1, q0:T], start=False, stop=False)
                kb.mm(pS, V(KrT.ap[0:64, ks], KrT.bufs), d["qrb"][0:64, q0:T], start=False, stop=(dgn < 0))
                if dgn >= 0:
                    kb.mm(pS, idb.all(), NEGA[:, dgn, q0:T], start=False, stop=True)
                pt = PT[kt % 4]
                kb.act(pt[:, q0:T], pS, AF.Exp, scale=SCALE)
                return pt, q0

            def pv(kt, pt, q0):
                st, sp = (kt == 0), (kt == nkt - 1)
                kb.mm(pO0.full(128, T - q0, q0), V(Vtok.ap[:, kt, 0:128], Vtok.bufs), pt[:, q0:T], start=st, stop=sp)
                kb.mm(pO1.full(128, T - q0, q0), V(Vtok.ap[:, kt, 128:256], Vtok.bufs), pt[:, q0:T], start=st, stop=sp)
                if kt == 0:
                    kb.copy(Pacc[:, q0:T], pt[:, q0:T])
                else:
                    kb.tt(Pacc[:, q0:T], Pacc[:, q0:T], pt[:, q0:T], ALU.add)

            LA = 2
            pend = {}
            for kt in range(nkt + LA):
                if kt < nkt:
                    pend[kt] = qk(kt)
                if kt - LA >= 0:
                    pv(kt - LA, *pend.pop(kt - LA))
            kb.copy(Pab.all(), Pacc.all(), eng="scalar")
            kb.mm(pD.full(128, T), oneb.all(), Pab.all())
            kb.recip(rden.all(), pD.full(128, T))
            kb.tt(ol[:, 0, :], pO0.full(128, T), rden.all(), ALU.mult)
            kb.tt(ol[:, 1, :], pO1.full(128, T), rden.all(), ALU.mult)
            pob = pfull()
            for j in range(2):
                kb.mm(pob, wuvv(j, h * 128, (h + 1) * 128), ol[:, j, :], start=(j == 0), stop=(j == 1))
            kb.copy(obT[:, h, :], pob, eng="scalar")
        if kind == "s":
            for s in range(2):
                items = []
                for pc in range(PAST // 1024):
                    Ksp, Krp, Vsp = spb[pc % 2]
                    cs = slice(pc * 1024, (pc + 1) * 1024)
                    for j in range(2):
                        kb.dma(Ksp[:, j, :], DI(ckvT[l, s, j * 128:(j + 1) * 128, cs]), eng="gpsimd")
                    kb.dma(Krp[0:64, :], DI(krT[l, s, :, cs]), eng="gpsimd")
                    kb.dma(Vsp.all(), DI(ckvN[l, s, cs, :].rearrange("(t q) c -> q t c", q=128)), eng="gpsimd")
                    for t in range(8):
                        ts_ = slice(t * 128, (t + 1) * 128)
                        items.append((Ksp[:, 0, ts_], Ksp[:, 1, ts_], Krp[0:64, ts_], Vsp[:, t, 0:128],
                                      Vsp[:, t, 128:256], 128))
                ss = slice(s * 32, (s + 1) * 32)
                items.append((KnT[:, 0, ss], KnT[:, 1, ss], KrnT[0:64, ss], Vn[0:32, s, 0:128], Vn[0:32, s, 128:256], 32))
                nit = len(items)
                for i, (k0, k1, kr_, v0, v1, rows) in enumerate(items):
                    pS = srot.next().full(rows, 256)
                    kb.mm(pS, k0, qlS[:, s * 2 + 0, :], start=True, stop=False)
                    kb.mm(pS, k1, qlS[:, s * 2 + 1, :], start=False, stop=False)
                    kb.mm(pS, kr_, qrS[0:64, s, :], start=False, stop=True)
                    pt = PT[i % 3]
                    kb.act(pt[0:rows, :], pS, AF.Exp, scale=SCALE)
                    st, sp = (i == 0), (i == nit - 1)
                    kb.mm(pO0.full(128, 256), v0, pt[0:rows, :], start=st, stop=sp)
                    kb.mm(pO1.full(128, 256), v1, pt[0:rows, :], start=st, stop=sp)
                    kb.mm(pD.full(128, 256), V(oneb.ap[0:rows, :], oneb.bufs), pt[0:rows, :], start=st, stop=sp)
                kb.recip(rden.all(), pD.full(128, 256))
                kb.tt(olS[:, 0, :], pO0.full(128, 256), rden.all(), ALU.mult)
                kb.tt(olS[:, 1, :], pO1.full(128, 256), rden.all(), ALU.mult)
                for h in range(H):
                    pob = pfull(128, 32)
                    for j in range(2):
                        kb.mm(pob, wuvv(j, h * 128, (h + 1) * 128), olS[:, j, h * 32:(h + 1) * 32],
                              start=(j == 0), stop=(j == 1))
                    kb.copy(obT[:, h, s * 32:(s + 1) * 32], pob, eng="scalar")
        for c in range(8):
            if c % 4 == 0:
                wgb = ws.get("ingb", c // 4)
                wob = ws.get("ob", c // 4)
            pg = pfull()
            proj(pg, wgb, (c % 4) * 128, (c % 4 + 1) * 128, xrhs, 8)
            kb.act(sg2.all(), pg, AF.Sigmoid)
            py = pfull()
            proj(py, wob, (c % 4) * 128, (c % 4 + 1) * 128, lambda kc: obT[:, kc, :], 8)
            kb.tt(tmpf.all(), py, sg2.all(), ALU.mult)
            kb.tt(mT[:, c, 0:T], mT[:, c, 0:T], tmpf.all(), ALU.add, eng="gpsimd")
        if phases <= 2:
            return

        def ln_tmps():
            return ([sal([128, T], BF16, name="ub") for _ in range(2)], [sal([128, T], BF16, name="sqc") for _ in range(2)],
                    sal([128, T], F32, name="mean"), sal([128, T], F32, name="rstd"))

        def layer_norm(u, tm, pm, pvb, gcol, bcol, out_bf, out_dram):
            ub, sqc, mean, rstd = tm
            kb.act(mean.all(), pm, AF.Copy, scale=1.0 / D)
            for c in range(8):
                kb.tt(u[:, c, :], u[:, c, :], mean.all(), ALU.subtract)
                kb.act(sqc[c % 2].all(), u[:, c, :], AF.Square)
                kb.mm(pvb, oneb.all(), sqc[c % 2].all(), start=(c == 0), stop=(c == 7))
            kb.act(rstd.all(), pvb, AF.Sqrt, bias=EPS, scale=1.0 / D)
            kb.recip(rstd.all(), rstd.all())
            for c in range(8):
                kb.tt(u[:, c, :], u[:, c, :], rstd.all(), ALU.mult)
                kb.act(u[:, c, :], u[:, c, :], AF.Identity, scale=prmv(l, gcol + c), bias=prmv(l, bcol + c))
                if out_bf is not None:
                    kb.copy(out_bf[:, c, 0:T], u[:, c, :], eng="gpsimd")
                if out_dram is not None:
                    kb.dma(xout(xdst[c * 128:(c + 1) * 128, :], c), u[:, c, :])

        phase_begin()
        u = sal([128, 8, T], F32, nb=8, name="u")
        xr = [sal([128, T], F32, name="xr") for _ in range(2)]
        tm = ln_tmps()
        phase_ready()
        fullrot = Rot([banks[0], banks[1], banks[2], banks[3], banks[4]])
        pfull = lambda rows=128, TT=None: fullrot.next().full(rows, TT or T)
        pm, pvb = banks[5].full(128, T), banks[6].full(128, T)
        for c in range(8):
            if c % 4 == 0:
                wo = ws.get("out", c // 4)
            po = pfull()
            proj(po, wo, (c % 4) * 128, (c % 4 + 1) * 128, lambda kc: mT[:, kc, 0:T], 8)
            kb.dma(xr[c % 2].all(), xin(xsrc[c * 128:(c + 1) * 128, :], c))
            kb.stt(u[:, c, :], xr[c % 2].all(), ALPHA, po, ALU.mult, ALU.add)
            kb.copy(tm[0][c % 2].all(), u[:, c, :], eng="scalar")
            kb.mm(pm, oneb.all(), tm[0][c % 2].all(), start=(c == 0), stop=(c == 7))
        layer_norm(u, tm, pm, pvb, PRM_LN1G, PRM_LN1B, xb, None)
        if phases <= 3:
            for c in range(8):
                kb.dma(xout(xdst[c * 128:(c + 1) * 128, :], c), u[:, c, :])
            return

        phase_begin(keep=[u])
        hid = sal([128, NFF, T], BF16, nb=NFF, name="hid")
        s1 = [sal([128, T], F32, name="s1") for _ in range(2)]
        tm = ln_tmps()
        phase_ready()
        x1rhs = lambda kc: xb[:, kc, 0:T]
        for j in range(NFF):
            if j % 2 == 0:
                wg = ws.get("gu", j // 2)
            b0 = (j % 2) * 256
            p1 = pfull()
            proj(p1, wg, b0, b0 + 128, x1rhs, 8)
            p3 = pfull()
            proj(p3, wg, b0 + 128, b0 + 256, x1rhs, 8)
            kb.act(s1[j % 2].all(), p1, AF.Silu)
            kb.tt(hid[:, j, :], p3, s1[j % 2].all(), ALU.mult)
        for c in range(8):
            wd = ws.get("down", c)
            po = pfull()
            proj(po, wd, 0, 128, lambda kc: hid[:, kc, :], NFF)
            kb.stt(u[:, c, :], u[:, c, :], ALPHA, po, ALU.mult, ALU.add)
            kb.copy(tm[0][c % 2].all(), u[:, c, :], eng="scalar")
            kb.mm(pm, oneb.all(), tm[0][c % 2].all(), start=(c == 0), stop=(c == 7))
        layer_norm(u, tm, pm, pvb, PRM_LN2G, PRM_LN2B, None, True)

    kb.block = block
    return nc, kb


_CACHE = {}


def _full_plan():
    plan = []
    for l in range(DEPTH):
        plan.append((l, "s", 0))
        for b in range(NBLK):
            plan.append((l, "p", b))
    return plan


def get_program(plan=None, phases=9):
    key = (tuple(plan) if plan else None, phases)
    if key not in _CACHE:
        nc, kb = build_program()
        for (l, kind, blk) in (plan or _full_plan()):
            kb.wstream.seq += kb.block_seq(l, phases)
        for (l, kind, blk) in (plan or _full_plan()):
            kb.block(l, kind, blk, phases)
        kb.P.finish()
        kb.P.emit()
        _CACHE[key] = (nc, kb)
    return _CACHE[key]


def kernel(x_prompt, x_sample, state_conv, state_gdn, cache_ckv, cache_krope, w_in, conv_w, a_log,
           dt_bias, gdn_norm_g, w_oa, q_norm_g, w_uq, kv_norm_g, w_ukv, w_ob, w_out, ln1_g, ln1_b,
           w_gu, w_down, ln2_g, ln2_b, _plan=None, _phases=9, _ncores=8):
    f = lambda a: np.ascontiguousarray(np.asarray(a, dtype=np.float32))
    x_prompt, x_sample, state_conv, state_gdn, cache_ckv, cache_krope = map(
        f, (x_prompt, x_sample, state_conv, state_gdn, cache_ckv, cache_krope))
    hw = host_weights(*map(f, (w_in, conv_w, a_log, dt_bias, gdn_norm_g, w_oa, q_norm_g, w_uq, kv_norm_g,
                               w_ukv, w_ob, w_out, ln1_g, ln1_b, w_gu, w_down, ln2_g, ln2_b)))
    hc = host_consts()
    nc, kb = get_program(_plan, _phases)
    ncores = _ncores
    in_maps = []
    for c in range(ncores):
        m = {}
        m["xT"] = np.ascontiguousarray(x_prompt[c % NSEQ].T)
        sb = [2 * c, 2 * c + 1]
        m["xsT"] = np.ascontiguousarray(x_sample[sb].reshape(2 * SL, D).T)
        sc = state_conv[:, sb]
        m["sconvT"] = np.ascontiguousarray(
            np.transpose(sc.reshape(DEPTH, 2, 3, 24, 128), (0, 1, 4, 3, 2)).reshape(DEPTH, 2, 128, 72))
        sg = state_gdn[:, sb]
        m["sgdn"] = np.ascontiguousarray(np.transpose(sg, (0, 1, 3, 2, 4)).reshape(DEPTH, 2, 128, 1024))
        ck = cache_ckv[:, sb]
        m["ckvN"] = np.ascontiguousarray(ck)
        m["ckvT"] = np.ascontiguousarray(np.transpose(ck, (0, 1, 3, 2)))
        m["krT"] = np.ascontiguousarray(np.transpose(cache_krope[:, sb], (0, 1, 3, 2)))
        for n in WP:
            m["w_" + n] = hw["w_" + n]
        m["prm"] = hw["prm"]
        m.update(hc)
        in_maps.append(m)
    res = run_bass_kernel_spmd(nc, in_maps, core_ids=list(range(ncores)))
    R = list(res.results)
    while len(R) < 8:
        R.append(R[0])
    ncores = 8
    yp = np.stack([R[c]["yT"].T for c in range(NSEQ)], 0)
    ys = np.concatenate([R[c]["ysT"].T.reshape(2, SL, D) for c in range(ncores)], 0)

    def conv_back(a):
        sh = a.shape[:-2]
        a = a.reshape(sh + (128, 24, 3))
        nd = len(sh)
        a = np.transpose(a, tuple(range(nd)) + (nd + 2, nd + 1, nd))
        return np.ascontiguousarray(a.reshape(sh + (3, 3072)))

    def gdn_back(a):
        sh = a.shape[:-2]
        a = a.reshape(sh + (128, 8, 128))
        nd = len(sh)
        return np.ascontiguousarray(np.transpose(a, tuple(range(nd)) + (nd + 1, nd, nd + 2)))

    p_conv = np.stack([conv_back(R[c]["p_conv"]) for c in range(NSEQ)], 1)
    p_gdn = np.stack([gdn_back(R[c]["p_gdn"]) for c in range(NSEQ)], 1)
    p_ckv = np.stack([np.transpose(R[c]["p_ckvT"], (0, 2, 1)) for c in range(NSEQ)], 1)
    p_kr = np.stack([np.transpose(R[c]["p_krT"], (0, 2, 1)) for c in range(NSEQ)], 1)
    s_conv = np.concatenate([conv_back(R[c]["s_conv"]) for c in range(ncores)], 1)
    s_gdn = np.concatenate([gdn_back(R[c]["s_gdn"]) for c in range(ncores)], 1)
    s_ckv = np.concatenate([np.transpose(R[c]["s_ckvT"], (0, 2, 1)).reshape(DEPTH, 2, SL, KVR)
                            for c in range(ncores)], 1)
    s_kr = np.concatenate([np.transpose(R[c]["s_krT"], (0, 2, 1)).reshape(DEPTH, 2, SL, ROPE)
                           for c in range(ncores)], 1)
    outs = (yp, ys, p_conv, p_gdn, p_ckv, p_kr, s_conv, s_gdn, s_ckv, s_kr)
    return tuple(np.ascontiguousarray(o, dtype=np.float32) for o in outs)
```
